# Optimizing a Trainium2 kernel written in Bass

```python
import math
import jax, jax.numpy as jnp
from jax import lax
import numpy as np

D_MODEL = 1024
BATCH = 32
SEQ = 2048
DEPTH = 1

ML_H = 4
ML_HD = D_MODEL // 8
ML_W = ML_H * ML_HD
ML_CONV = 4
ML_CHUNK = 64
ATT_H = 8
ATT_KV = 2
ATT_HD = 64
ATT_W = ATT_H * ATT_HD
WINDOW = 128
ATT_BLOCK = WINDOW
ROPE_THETA = 10000.0
MIX_W = ML_W + ATT_W
IN_COLS = 4 * ML_W + 2 * ML_H + ATT_W + 2 * ATT_KV * ATT_HD
D_FF = 2816
FFN_CONV = 3
N_MOD = 6
RMS_EPS = 1e-6

kernel_name = "hybrid_mlstm_swa_convffn_layer"


def rms_norm(x, g):
    xf = x.astype(jnp.float32)
    y = xf * lax.rsqrt(jnp.mean(xf * xf, axis=-1, keepdims=True) + RMS_EPS)
    return (y * g.astype(jnp.float32)).astype(x.dtype)


def causal_dwconv(x, w, b):
    k_len, ch = w.shape
    y = lax.conv_general_dilated(
        x, w[:, None, :].astype(x.dtype), window_strides=(1,),
        padding=((k_len - 1, 0),), dimension_numbers=("NWC", "WIO", "NWC"),
        feature_group_count=ch)
    return y + b.astype(x.dtype)


def rope(x, pos):
    half = x.shape[-1] // 2
    inv = ROPE_THETA ** (-jnp.arange(half, dtype=jnp.float32) / half)
    ang = pos.astype(jnp.float32)[:, None] * inv[None, :]
    cos = jnp.cos(ang)[None, :, None, :]
    sin = jnp.sin(ang)[None, :, None, :]
    xf = x.astype(jnp.float32)
    x1, x2 = xf[..., :half], xf[..., half:]
    return jnp.concatenate([x1 * cos - x2 * sin, x2 * cos + x1 * sin], axis=-1)


def sliding_window_attention(q, k, v, sinks):
    B, S, H, D = q.shape
    G = k.shape[2]
    R = H // G
    nb = S // ATT_BLOCK
    qb = q.reshape(B, nb, ATT_BLOCK, G, R, D)
    kb = k.reshape(B, nb, ATT_BLOCK, G, D)
    vb = v.reshape(B, nb, ATT_BLOCK, G, D)
    pad = ((0, 0), (1, 0), (0, 0), (0, 0), (0, 0))
    kband = jnp.concatenate([jnp.pad(kb, pad)[:, :-1], kb], axis=2)
    vband = jnp.concatenate([jnp.pad(vb, pad)[:, :-1], vb], axis=2)
    s = jnp.einsum('bnqgrd,bnkgd->bngrqk', qb, kband) * (1.0 / math.sqrt(D))
    qi = jnp.arange(ATT_BLOCK)[:, None] + ATT_BLOCK
    kj = jnp.arange(2 * ATT_BLOCK)[None, :]
    rel = qi - kj
    band = (rel >= 0) & (rel < WINDOW)
    kpos = jnp.arange(nb)[:, None, None] * ATT_BLOCK + kj[None] - ATT_BLOCK
    valid = band[None] & (kpos >= 0)
    s = jnp.where(valid[None, :, None, None], s, -jnp.inf)
    sk = sinks.astype(jnp.float32).reshape(G, R)[None, None, :, :, None, None]
    m = jnp.maximum(jnp.max(s, axis=-1, keepdims=True), sk)
    p = jnp.exp(s - m)
    denom = jnp.sum(p, axis=-1, keepdims=True) + jnp.exp(sk - m)
    p = p / denom
    o = jnp.einsum('bngrqk,bnkgd->bnqgrd', p, vband)
    return o.reshape(B, S, H, D)


def mlstm_chunkwise(q, k, v, i_pre, logf):
    B, S, H, D = q.shape
    L = ML_CHUNK
    nc = S // L
    to_c = lambda t: t.reshape(B, nc, L, H, D).transpose(1, 0, 3, 2, 4)
    to_cg = lambda t: t.reshape(B, nc, L, H).transpose(1, 0, 3, 2)
    causal = jnp.tril(jnp.ones((L, L), dtype=bool))

    def step(carry, xs):
        C, n, m = carry
        qc, kc, vc, ic, fc = xs
        b = jnp.cumsum(fc, axis=-1)
        dmat = jnp.where(causal, b[..., :, None] - b[..., None, :] + ic[..., None, :], -jnp.inf)
        m_inter = b + m[..., None]
        m_t = jnp.maximum(m_inter, jnp.max(dmat, axis=-1))
        w_inter = jnp.exp(m_inter - m_t)
        sqk = jnp.einsum('bhtd,bhsd->bhts', qc, kc) * jnp.exp(dmat - m_t[..., None])
        num = jnp.einsum('bhts,bhsd->bhtd', sqk, vc) + \
            w_inter[..., None] * jnp.einsum('bhvk,bhtk->bhtv', C, qc)
        den = jnp.sum(sqk, axis=-1) + w_inter * jnp.einsum('bhk,bhtk->bht', n, qc)
        h = num / jnp.maximum(jnp.abs(den), jnp.exp(-m_t))[..., None]
        b_last = b[..., -1]
        g = b_last[..., None] - b + ic
        m_new = jnp.maximum(b_last + m, jnp.max(g, axis=-1))
        decay = jnp.exp(b_last + m - m_new)
        ws = jnp.exp(g - m_new[..., None])
        C_new = decay[..., None, None] * C + jnp.einsum('bhsv,bhsk->bhvk', vc * ws[..., None], kc)
        n_new = decay[..., None] * n + jnp.einsum('bhs,bhsk->bhk', ws, kc)
        return (C_new, n_new, m_new), h

    init = (jnp.zeros((B, H, D, D), jnp.float32), jnp.zeros((B, H, D), jnp.float32),
            jnp.zeros((B, H), jnp.float32))
    _, hs = lax.scan(step, init, (to_c(q), to_c(k), to_c(v), to_cg(i_pre), to_cg(logf)))
    return hs.transpose(1, 0, 3, 2, 4).reshape(B, S, H, D)


def token_mixer(h, w_in, ml_conv_w, ml_conv_b, ml_i_b, ml_f_b, ml_norm_g,
                attn_sinks, attn_norm_g, w_out):
    B, S, _ = h.shape
    proj = h @ w_in
    cuts = [2 * ML_W, 3 * ML_W, 4 * ML_W, 4 * ML_W + 2 * ML_H, 4 * ML_W + 2 * ML_H + ATT_W]
    qk_ml, v_ml, o_ml, gates, q_at, kv_at = jnp.split(proj, cuts, axis=-1)
    qk_ml = jax.nn.silu(causal_dwconv(qk_ml, ml_conv_w, ml_conv_b)).astype(jnp.float32)
    q_ml = qk_ml[..., :ML_W].reshape(B, S, ML_H, ML_HD)
    k_ml = qk_ml[..., ML_W:].reshape(B, S, ML_H, ML_HD) * (1.0 / math.sqrt(ML_HD))
    v_ml = v_ml.astype(jnp.float32).reshape(B, S, ML_H, ML_HD)
    gates = gates.astype(jnp.float32)
    i_pre = gates[..., :ML_H] + ml_i_b.astype(jnp.float32)
    logf = jax.nn.log_sigmoid(gates[..., ML_H:] + ml_f_b.astype(jnp.float32))
    h_ml = mlstm_chunkwise(q_ml, k_ml, v_ml, i_pre, logf)
    h_ml = rms_norm(h_ml, ml_norm_g)
    h_ml = h_ml * jax.nn.sigmoid(o_ml.astype(jnp.float32).reshape(B, S, ML_H, ML_HD))
    h_ml = h_ml.reshape(B, S, ML_W).astype(h.dtype)
    pos = jnp.arange(S)
    q_at = rope(q_at.reshape(B, S, ATT_H, ATT_HD), pos)
    k_at = rope(kv_at[..., :ATT_KV * ATT_HD].reshape(B, S, ATT_KV, ATT_HD), pos)
    v_at = kv_at[..., ATT_KV * ATT_HD:].astype(jnp.float32).reshape(B, S, ATT_KV, ATT_HD)
    o_at = sliding_window_attention(q_at, k_at, v_at, attn_sinks).reshape(B, S, ATT_W)
    o_at = rms_norm(o_at, attn_norm_g).astype(h.dtype)
    return jnp.concatenate([h_ml, o_at], axis=-1) @ w_out


def conv_ffn(h, w_up, conv_w, conv_b, w_down):
    u = causal_dwconv(h @ w_up, conv_w, conv_b)
    g, val = u[..., :D_FF], u[..., D_FF:]
    return (jax.nn.gelu(g, approximate=True) * val) @ w_down


def setup_inputs(seed: int = 0) -> dict:
    key = jax.random.key(seed)
    ks = jax.random.split(key, 24)
    f32 = jnp.float32
    nrm = lambda k, shp, s: jax.random.normal(k, shp, f32) * s
    gain = lambda k, shp: 1.0 + 0.05 * jax.random.normal(k, shp, f32)
    L = DEPTH
    f_bias = jnp.linspace(3.0, 6.0, ML_H, dtype=f32)[None, :] + nrm(ks[7], (L, ML_H), 0.1)
    return {
        "x": nrm(ks[0], (BATCH, SEQ, D_MODEL), 1.0),
        "c": nrm(ks[1], (BATCH, D_MODEL), 1.0),
        "w_ada": nrm(ks[2], (L, D_MODEL, N_MOD * D_MODEL), 0.5 * D_MODEL ** -0.5),
        "b_ada": nrm(ks[3], (L, N_MOD * D_MODEL), 0.02),
        "pre_mix_g": gain(ks[4], (L, D_MODEL)),
        "w_in": nrm(ks[5], (L, D_MODEL, IN_COLS), D_MODEL ** -0.5),
        "ml_conv_w": nrm(ks[6], (L, ML_CONV, 2 * ML_W), ML_CONV ** -0.5),
        "ml_conv_b": nrm(ks[8], (L, 2 * ML_W), 0.02),
        "ml_i_b": nrm(ks[9], (L, ML_H), 0.1),
        "ml_f_b": f_bias,
        "ml_norm_g": gain(ks[10], (L, ML_H, ML_HD)),
        "attn_sinks": nrm(ks[11], (L, ATT_H), 1.0),
        "attn_norm_g": gain(ks[12], (L, ATT_W)),
        "w_out": nrm(ks[13], (L, MIX_W, D_MODEL), MIX_W ** -0.5),
        "post_mix_g": gain(ks[14], (L, D_MODEL)),
        "pre_ffn_g": gain(ks[15], (L, D_MODEL)),
        "w_up": nrm(ks[16], (L, D_MODEL, 2 * D_FF), D_MODEL ** -0.5),
        "ffn_conv_w": nrm(ks[17], (L, FFN_CONV, 2 * D_FF), FFN_CONV ** -0.5),
        "ffn_conv_b": nrm(ks[18], (L, 2 * D_FF), 0.02),
        "w_down": nrm(ks[19], (L, D_FF, D_MODEL), D_FF ** -0.5),
        "post_ffn_g": gain(ks[20], (L, D_MODEL)),
    }


def reference(x, c, w_ada, b_ada, pre_mix_g, w_in, ml_conv_w, ml_conv_b, ml_i_b,
              ml_f_b, ml_norm_g, attn_sinks, attn_norm_g, w_out, post_mix_g,
              pre_ffn_g, w_up, ffn_conv_w, ffn_conv_b, w_down, post_ffn_g):
    c_act = jax.nn.silu(c)
    for l in range(DEPTH):
        mod = c_act @ w_ada[l] + b_ada[l]
        sh1, sc1, g1, sh2, sc2, g2 = [t[:, None, :] for t in jnp.split(mod, N_MOD, axis=-1)]
        h = rms_norm(x, pre_mix_g[l]) * (1.0 + sc1) + sh1
        y = token_mixer(h, w_in[l], ml_conv_w[l], ml_conv_b[l], ml_i_b[l], ml_f_b[l],
                        ml_norm_g[l], attn_sinks[l], attn_norm_g[l], w_out[l])
        x = x + g1 * rms_norm(y, post_mix_g[l])
        h = rms_norm(x, pre_ffn_g[l]) * (1.0 + sc2) + sh2
        y = conv_ffn(h, w_up[l], ffn_conv_w[l], ffn_conv_b[l], w_down[l])
        x = x + g2 * rms_norm(y, post_ffn_g[l])
    return x
```

```python
import math
from contextlib import ExitStack

import numpy as np
import ml_dtypes

import concourse.bass as bass
import concourse.mybir as mybir
from concourse.bass_utils import run_bass_kernel_spmd

F32 = mybir.dt.float32
BF16 = mybir.dt.bfloat16
AF = mybir.ActivationFunctionType
ALU = mybir.AluOpType

NCORES = 8
D = 1024
DFF = 2816
NCH = DFF // 128
INC = 2824
EPS = 1e-6
KSCALE = 1.0 / math.sqrt(128.0)
NEG = -30000.0

SEM_LIMIT = 24000
ENGS = ("pe", "act", "dve", "pool", "sp")


class Buf:
    __slots__ = ("name", "last_w", "readers", "excl")

    def __init__(self, name, excl=False):
        self.name = name
        self.last_w = None
        self.readers = []
        self.excl = excl


class Op:
    __slots__ = ("eng", "fn", "deps", "is_dma", "needed", "waits", "tok")

    def __init__(self, eng, fn, deps, is_dma):
        self.eng = eng
        self.fn = fn
        self.deps = deps
        self.is_dma = is_dma
        self.needed = False
        self.waits = []
        self.tok = None


class Prog:
    def __init__(self, nc, n_dma_sems=12):
        self.nc = nc
        self.ops = []
        self.n_dma_sems = n_dma_sems

    def op(self, eng, fn, reads=(), writes=(), dma=False):
        ex = [b for b in reads if b.excl]
        if ex:
            writes = list(writes) + [b for b in ex if b not in writes]
            reads = [b for b in reads if not b.excl]
        deps = set()
        for b in reads:
            if b.last_w is not None:
                deps.add(b.last_w)
        for b in writes:
            if b.last_w is not None:
                deps.add(b.last_w)
            deps.update(b.readers)
        oid = len(self.ops)
        self.ops.append(Op(eng, fn, deps, dma))
        for b in writes:
            b.last_w = oid
            b.readers = []
        for b in reads:
            if b not in writes:
                b.readers.append(oid)
        return oid

    def emit(self, block, sems_cm):
        nc = self.nc
        ops = self.ops
        n = len(ops)
        eng_ops = {e: [] for e in ENGS}
        for i, o in enumerate(ops):
            eng_ops[o.eng].append(i)
        pos = {}
        for e in ENGS:
            k = 0
            for i in eng_ops[e]:
                if not ops[i].is_dma:
                    pos[i] = k
                    k += 1
        vc = [None] * n
        known = {e: {x: -1 for x in ENGS} for e in ENGS}
        known_dma = {e: set() for e in ENGS}
        for i, o in enumerate(ops):
            kc = known[o.eng]
            kd = known_dma[o.eng]
            for d in sorted(o.deps):
                od = ops[d]
                if od.is_dma:
                    if d in kd:
                        continue
                    o.waits.append(d)
                    kd.add(d)
                else:
                    if od.eng == o.eng and o.eng == "pe":
                        continue
                    if kc[od.eng] >= pos[d]:
                        continue
                    o.waits.append(d)
                    od.needed = True
                dc = vc[d]
                for x in ENGS:
                    if dc[x] > kc[x]:
                        kc[x] = dc[x]
            c = dict(kc)
            if not o.is_dma:
                if pos[i] > c[o.eng]:
                    c[o.eng] = pos[i]
            vc[i] = c

        def new_sem(name):
            return sems_cm.enter_context(nc.semaphore(name))

        for e in ENGS:
            cur = None
            cnt = 0
            k = 0
            for i in eng_ops[e]:
                o = ops[i]
                if o.is_dma or not o.needed:
                    continue
                if cur is None or cnt >= SEM_LIMIT:
                    cur = new_sem(f"s_{e}_{k}_{id(self) % 997}")
                    k += 1
                    cnt = 0
                cnt += 1
                o.tok = (cur, cnt)
        caps = {"sp": 10, "pool": 12, "act": 4, "dve": 2, "pe": 2}
        for e in ENGS:
            dl = [i for i in eng_ops[e] if ops[i].is_dma]
            if not dl:
                continue
            nsem = min(len(dl), caps[e])
            dsems = [new_sem(f"s_dma_{e}_{j}_{id(self) % 997}") for j in range(nsem)]
            dcnt = [0] * nsem
            dprev = [None] * nsem
            for j, i in enumerate(dl):
                o = ops[i]
                s = j % nsem
                if dprev[s] is not None and dprev[s] not in o.waits:
                    o.waits.append(dprev[s])
                dcnt[s] += 16
                o.tok = (dsems[s], dcnt[s])
                dprev[s] = i
        self.stats = {e: len(eng_ops[e]) for e in ENGS}
        self.stats["waits"] = sum(len(o.waits) for o in ops)
        self.stats["incs"] = sum(1 for o in ops if o.tok is not None)

        def run_engine(e, eng):
            for i in eng_ops[e]:
                o = ops[i]
                seen = {}
                for d in o.waits:
                    sem, val = ops[d].tok
                    key = id(sem)
                    if key not in seen or seen[key][1] < val:
                        seen[key] = (sem, val)
                for sem, val in seen.values():
                    eng.wait_ge(sem, val)
                ins = o.fn(eng)
                if o.tok is not None and ins is not None:
                    ins.then_inc(o.tok[0], 16 if o.is_dma else 1)

        @block.tensor
        def _(eng):
            run_engine("pe", eng)

        @block.scalar
        def _(eng):
            run_engine("act", eng)

        @block.vector
        def _(eng):
            run_engine("dve", eng)

        @block.gpsimd
        def _(eng):
            run_engine("pool", eng)

        @block.sync
        def _(eng):
            run_engine("sp", eng)


class K:
    def __init__(self, P):
        self.P = P

    def mm(self, out, lhsT, rhs, start, stop, R, W):
        self.P.op("pe", lambda e: e.matmul(out, lhsT=lhsT, rhs=rhs, start=start, stop=stop), R, W)

    def tr(self, out, in_, ident, R, W):
        self.P.op("pe", lambda e: e.transpose(out, in_, ident), R, W)

    def act(self, out, in_, func, R, W, scale=1.0, bias=None, accum=None):
        def fn(e):
            kw = {}
            if bias is not None:
                kw["bias"] = bias
            if accum is not None:
                kw["accum_out"] = accum
            return e.activation(out=out, in_=in_, func=func, scale=scale, **kw)
        self.P.op("act", fn, R, W)

    def ts(self, eng, out, in0, s1, s2, op0, op1, R, W):
        def fn(e):
            if op1 is None:
                return e.tensor_scalar(out=out, in0=in0, scalar1=s1, scalar2=None, op0=op0)
            return e.tensor_scalar(out=out, in0=in0, scalar1=s1, scalar2=s2, op0=op0, op1=op1)
        self.P.op(eng, fn, R, W)

    def stt(self, out, in0, scalar, in1, op0, op1, R, W):
        self.P.op("dve", lambda e: e.scalar_tensor_tensor(out=out, in0=in0, scalar=scalar, in1=in1,
                                                          op0=op0, op1=op1), R, W)

    def tt(self, eng, out, in0, in1, op, R, W):
        self.P.op(eng, lambda e: e.tensor_tensor(out=out, in0=in0, in1=in1, op=op), R, W)

    def cp(self, eng, out, in_, R, W):
        if eng == "act":
            self.P.op("act", lambda e: e.activation(out=out, in_=in_, func=AF.Copy), R, W)
        else:
            self.P.op(eng, lambda e: e.tensor_copy(out=out, in_=in_), R, W)

    def recip(self, out, in_, R, W):
        self.P.op("dve", lambda e: e.reciprocal(out=out, in_=in_), R, W)

    def memset(self, eng, ap, val, W):
        self.P.op(eng, lambda e: e.memset(ap, val), (), W)

    def dma(self, eng, out, in_, R, W):
        self.P.op(eng, lambda e: e.dma_start(out=out, in_=in_), R, W, dma=True)

    def fence(self, eng, R):
        self.P.op(eng, lambda e: None, R, ())


def build(NSEQ, S):
    NT = NSEQ * S
    NB = S // 128
    TPS = S // 512
    nc = bass.Bass("TRN2", target_bir_lowering=False)

    def din(name, shape, dt=F32):
        return nc.dram_tensor(name, list(shape), dt, kind="ExternalInput").ap()

    x_d = din("x", [NT, D])
    cT_d = din("cT", [128, 8, NSEQ])
    wada_d = din("w_ada", [D, 6 * D])
    bada_d = din("b_ada", [1, 6 * D])
    win_d = din("w_in", [D, INC])
    wout_d = din("w_out", [D, D])
    wup_d = din("w_up", [D, 2 * DFF])
    wdn_d = din("w_down", [DFF, D])
    pmg_d = din("pmg", [128, 8])
    pfg_d = din("pfg", [128, 8])
    post4_d = din("post4", [NSEQ, 2, D])
    mlcw_d = din("mlcw", [128, 8, 4])
    mlcb_d = din("mlcb", [128, 8])
    ib_d = din("ib16", [128, 16])
    fb_d = din("fb16", [128, 16])
    mlg_d = din("mlg", [128, 512])
    sink_d = din("sinks", [128, 8])
    attg_d = din("attg", [128, 512])
    ffw_d = din("ffw", [128, 2 * NCH, 3])
    ffb_d = din("ffb", [128, 2 * NCH])
    identb_d = din("identb", [128, 128], BF16)
    tri_d = din("triu", [128, 128])
    maskU_d = din("masku", [128, 128])
    maskC_d = din("maskc", [128, 512], BF16)
    maskP_d = din("maskp", [128, 512], BF16)
    cos_d = din("cos", [128, NB, 32])
    sin_d = din("sin", [128, NB, 32])
    i4_d = din("i4", [4, 4])
    out_d = nc.dram_tensor("out", [NT, D], F32, kind="ExternalOutput").ap()
    x1s = nc.dram_tensor("x1s", [NT, D], F32).ap()
    gsc = nc.dram_tensor("gsc", [2, NSEQ, D], F32).ap()

    with ExitStack() as outer:
        def sb(name, shape, dt=F32):
            return outer.enter_context(nc.sbuf_tensor("s_" + name, list(shape), dt))

        G2 = sb("G2", [128, 8, NSEQ])
        SH2 = sb("SH2", [128, 8, NSEQ])
        identb = sb("identb", [128, 128], BF16)
        neghalf = sb("neghalf", [128, 16])
        banks = [outer.enter_context(nc.psum_tensor(f"bank{i}", [128, 512], F32)) for i in range(8)]

        with ExitStack() as ph:
            def sb1(name, shape, dt=F32):
                return ph.enter_context(nc.sbuf_tensor("t_" + name, list(shape), dt))

            P = Prog(nc)
            k = K(P)
            B = lambda n: Buf(n)

            Win = sb1("Win", [128, 8, INC], BF16)
            Wout = sb1("Wout", [128, 8, D], BF16)
            b_Win = [B(f"Win{i}") for i in range(8)]
            b_Wout = [B(f"Wout{i}") for i in range(8)]
            for i in range(8):
                k.dma("pool", Win[:, i, :], win_d[i * 128:(i + 1) * 128, :], (), [b_Win[i]])
            for i in range(8):
                k.dma("pool", Wout[:, i, :], wout_d[i * 128:(i + 1) * 128, :], (), [b_Wout[i]])

            RC = []
            b_cT = B("c_cT")
            cT = sb1("cT", [128, 8, NSEQ])
            k.dma("sp", cT[:], cT_d, (), [b_cT])
            b_i4 = B("c_i4")
            i4 = sb1("i4", [4, 4])
            k.dma("sp", i4[:], i4_d, (), [b_i4])
            onesf = sb1("onesf", [128, 128])
            onesb = sb1("onesb", [128, 2], BF16)
            b_c2 = B("c_misc")
            k.memset("dve", onesf[:], 1.0, [b_c2])
            k.memset("dve", onesb[:], 1.0, [b_c2])
            k.memset("dve", neghalf[:], -0.5, [b_c2])
            b_bk = [Buf(f"bk{i}", excl=True) for i in range(8)]
            RC0 = [b_cT, b_i4, b_c2]
            consts = {}

            def load_consts():
                b_id = B("c_ident")
                RC.append(b_id)
                k.dma("sp", identb[:], identb_d, (), [b_id])
                for (nm, src, shp, dt) in ():
                    pass
            const_srcs = [(nm, src) for (nm, src, shp, dt) in (("pmg", pmg_d, [128, 8], F32), ("pfg", pfg_d, [128, 8], F32),
                                           ("mlcw", mlcw_d, [128, 8, 4], F32), ("mlcb", mlcb_d, [128, 8], F32),
                                           ("ib16", ib_d, [128, 16], F32), ("fb16", fb_d, [128, 16], F32),
                                           ("mlgh", mlg_d, [128, 512], F32), ("esink", sink_d, [128, 8], F32),
                                           ("attg", attg_d, [128, 512], F32), ("triu", tri_d, [128, 128], F32),
                                           ("masku", maskU_d, [128, 128], F32), ("maskc", maskC_d, [128, 512], BF16),
                                           ("maskp", maskP_d, [128, 512], BF16), ("cosT", cos_d, [128, NB, 32], F32),
                                           ("sinT", sin_d, [128, NB, 32], F32))]

            def load_some_consts(cnt):
                for _ in range(cnt):
                    if const_srcs:
                        nm, src = const_srcs.pop(0)
                        k.dma("sp", consts[nm][:], src, (), [consts["b_" + nm]])

            for (nm, shp, dt) in (("pmg", [128, 8], F32), ("pfg", [128, 8], F32), ("mlcw", [128, 8, 4], F32),
                                  ("mlcb", [128, 8], F32), ("ib16", [128, 16], F32), ("fb16", [128, 16], F32),
                                  ("mlgh", [128, 512], F32), ("esink", [128, 8], F32), ("attg", [128, 512], F32),
                                  ("triu", [128, 128], F32), ("masku", [128, 128], F32), ("maskc", [128, 512], BF16),
                                  ("maskp", [128, 512], BF16), ("cosT", [128, NB, 32], F32), ("sinT", [128, NB, 32], F32)):
                consts[nm] = sb1(nm, shp, dt)
                consts["b_" + nm] = B("c_" + nm)
                RC.append(consts["b_" + nm])
            pmg, pfg, mlcw, mlcb, ib16, fb16 = (consts[n_] for n_ in ("pmg", "pfg", "mlcw", "mlcb", "ib16", "fb16"))
            mlgh, esink, attg, triu, masku = (consts[n_] for n_ in ("mlgh", "esink", "attg", "triu", "masku"))
            maskc, maskp, cosT, sinT = (consts[n_] for n_ in ("maskc", "maskp", "cosT", "sinT"))
            RC += RC0
            sT = sb1("sT", [128, 8, NSEQ])
            b_sT = B("sT")
            k.act(sT[:], cT[:], AF.Tanh, RC0, [b_sT], scale=0.5)
            k.ts("dve", sT[:], sT[:], 0.5, 0.5, ALU.mult, ALU.add, [b_sT], [b_sT])
            k.tt("dve", sT[:], sT[:], cT[:], ALU.mult, [b_sT] + RC0, [b_sT])
            modc = sb1("modc", [NSEQ, 1, 512])
            b_mc = B("modc")
            b_modc = [b_mc, b_mc]
            badar = sb1("badar", [1, 512])
            postc = sb1("postc", [NSEQ, 512])
            wring = sb1("wring", [128, 2, 2, 512])
            b_wr = [B("wr0"), B("wr1")]
            b_bd = B("bd")
            b_pc = B("pc")
            pmt = banks[3][:, 0:32 * NSEQ].rearrange("p (c b) -> p c b", b=NSEQ)
            b_gsc = [[B(f"gsc{i}_{q}") for q in range(2)] for i in range(2)]
            grp = {0: 0, 1: 1, 3: 2, 4: 3}
            for n in range(12):
                m_ = n % 2
                c0 = n * 512
                g1024 = n // 2
                pm = banks[1 + m_][0:NSEQ, :]
                k.dma("sp", badar[:], bada_d[0:1, c0:c0 + 512], (), [b_bd])
                for kp in range(4):
                    s_ = (n * 4 + kp) % 2
                    k.dma("sp", wring[:, s_, :, :],
                          wada_d[kp * 256:(kp + 1) * 256, c0:c0 + 512].rearrange("(k p) n -> p k n", p=128), (), [b_wr[s_]])
                    for kk in range(2):
                        k.mm(pm, sT[:, kp * 2 + kk, :], wring[:, s_, kk, :], kp == 0 and kk == 0, False,
                             [b_sT, b_wr[s_]], [b_bk[1 + m_]])
                if n == 0:
                    load_consts()
                load_some_consts(2)
                k.mm(pm, onesf[0:1, 0:NSEQ], badar[0:1, :], False, True, RC0 + [b_bd], [b_bk[1 + m_]])
                k.cp("act", modc[:, 0, :], pm, [b_bk[1 + m_]], [b_modc[m_]])
                if g1024 in grp:
                    for c in range(4):
                        k.mm(pmt[:, grp[g1024] * 8 + m_ * 4 + c, :], modc[:, 0, c * 128:(c + 1) * 128],
                             i4[0:NSEQ, 0:NSEQ], True, True, [b_modc[m_]] + RC0, [b_bk[3]])
                else:
                    gi = 0 if g1024 == 2 else 1
                    k.dma("sp", postc[:], post4_d[:, gi, m_ * 512:(m_ + 1) * 512], (), [b_pc])
                    k.tt("dve", modc[:, 0, :], modc[:, 0, :], postc[:], ALU.mult, [b_modc[m_], b_pc], [b_modc[m_]])
                    k.dma("sp", gsc[gi][:, m_ * 512:(m_ + 1) * 512], modc[:, 0, :], [b_modc[m_]], [b_gsc[gi][m_]])
            b_c3 = B("c_esink2")
            k.ts("dve", mlgh[:], mlgh[:], 0.5, None, ALU.mult, None, [consts["b_mlgh"]], [consts["b_mlgh"]])
            k.act(esink[:], esink[:], AF.Exp, [consts["b_esink"]], [consts["b_esink"]])
            G1 = sb1("G1", [128, 8, NSEQ])
            SH1 = sb1("SH1", [128, 8, NSEQ])
            b_G = B("G")
            k.cp("dve", SH1[:], pmt[:, 0:8, :], [b_bk[3]], [b_G])
            k.stt(G1[:], pmt[:, 8:16, :], 1.0, pmg[:].unsqueeze(2).broadcast_to([128, 8, NSEQ]),
                  ALU.add, ALU.mult, [b_bk[3]] + RC, [b_G])
            k.cp("dve", SH2[:], pmt[:, 16:24, :], [b_bk[3]], [b_G])
            k.stt(G2[:], pmt[:, 24:32, :], 1.0, pfg[:].unsqueeze(2).broadcast_to([128, 8, NSEQ]),
                  ALU.add, ALU.mult, [b_bk[3]] + RC, [b_G])

            XB = sb1("XB", [128, 8, D])
            b_XB = [B(f"XB{i}") for i in range(8)]
            GP1 = sb1("GP1", [128, 1, D])
            b_GP1 = B("GP1")
            hT = sb1("hT", [128, 8, 512], BF16)
            b_hT = [B(f"hT{j}") for j in range(4)]
            qkT = sb1("qkT", [128, 8, 512], BF16)
            b_qkT = [B(f"qkT{c}") for c in range(8)]
            U = sb1("U", [128, 2, 515])
            b_U = [B("U0"), B("U1")]
            acc = sb1("acc", [128, 2, 512])
            b_acc = [B("acc0"), B("acc1")]
            halo = sb1("halo", [128, 8, 3])
            b_halo = B("halo")
            xn = sb1("xn", [128, 2, D], BF16)
            b_xn = [B("xn0"), B("xn1")]
            junk = sb1("junk", [128, D], BF16)
            b_junk = B("junk")
            st = sb1("st", [128, 16])
            b_st = B("st")
            gat = sb1("gat", [128, 8, 16])
            b_gat = B("gat")
            sd = sb1("sd", [128, 16])
            eab = sb1("eab", [128, 16], BF16)
            tanho = sb1("tanho", [128, 512], BF16)
            b_tanho = B("tanho")
            gso = sb1("gso", [128, 4, 512], BF16)
            b_gso = [B(f"gso{j}") for j in range(4)]
            vtil = sb1("vtil", [128, 4, 4, 128], BF16)
            b_vtil = [B(f"vtil{j}") for j in range(4)]
            STm = sb1("STm", [128, 4, 4, 128], BF16)
            b_STm = [B(f"STm{j}") for j in range(4)]
            ktok = sb1("ktok", [128, 4, 4, 128], BF16)
            b_ktok = [B(f"ktok{j}") for j in range(4)]
            CT = sb1("CT", [128, 4, 128])
            CTb = sb1("CTb", [128, 4, 128], BF16)
            nv = sb1("nv", [128, 4])
            nvt = sb1("nvt", [128, 4])
            nvb = sb1("nvb", [128, 4], BF16)
            b_CT = B("CT")
            b_CTb = B("CTb")
            b_nv = B("nv")
            b_nvb = B("nvb")
            mst = sb1("mst", [128, 2, 8, 4])
            b_mst = [B("mst0"), B("mst1")]
            rt1 = sb1("rt1", [128, 640])
            rt2 = sb1("rt2", [128, 640])
            rot = sb1("rot", [128, 4, 640], BF16)
            b_rt1, b_rt2 = B("rt1"), B("rt2")
            b_rot = [B(f"rot{j}") for j in range(4)]
            qTa = sb1("qTa", [128, 4, 4, 128], BF16)
            b_qTa = [B(f"qTa{j}") for j in range(4)]
            kTr = sb1("kTr", [128, 8, 128], BF16)
            b_kTr = [B(f"kTr{i}") for i in range(8)]
            vat = sb1("vat", [128, 8, 128], BF16)
            b_vat = [B(f"vat{i}") for i in range(8)]
            pT = sb1("pT", [128, 4, 512], BF16)
            b_pT = [B(f"pT{i}") for i in range(4)]
            ast = sb1("ast", [128, 2, 32])
            b_ast = [B("ast0"), B("ast1")]
            osb = sb1("osb", [128, 1, 512])
            b_os = B("osb")
            b_osb = [b_os, b_os]
            mix = sb1("mix", [128, 2, D], BF16)
            b_mixm, b_mixa = [B("mixm0"), B("mixm1")], [B("mixa0"), B("mixa1")]
            mixT = sb1("mixT", [128, 2, 8, 128], BF16)
            b_mixT = [B("mixT0"), B("mixT1")]
            yst = sb1("yst", [128, 2, 8])
            b_yst = [B("yst0"), B("yst1")]

            TR = banks[0][:].bitcast(BF16).rearrange("p (k n) -> p k n", k=8)
            b_bk0a = b_bk0b = b_bk[0]
            b_bk1, b_bk2, b_bk3, b_bk4, b_bk5, b_bk7 = b_bk[1], b_bk[2], b_bk[3], b_bk[4], b_bk[5], b_bk[7]
            QK = [banks[1][:], banks[2][:]]
            A0, A1 = banks[3][:], banks[4][:]
            Bt0, Bt1 = banks[5][:], banks[6][:, 0:256]
            b_bt1 = b_bk[6]
            GT = banks[6][:, 256:288].rearrange("p (j g) -> p j g", g=8)
            CSa = banks[6][:, 288:304]
            CSb = banks[6][:, 304:320]
            DN = banks[6][:, 320:324]
            DNs = banks[6][:, 324:328]
            SD = banks[6][:, 328:336]
            RTk = banks[6][:, 336:400].bitcast(BF16)
            b_GT = b_CS = b_DN = b_DNs = b_SD = b_RTk = b_bk[6]
            SC = banks[7][:]
            ST = banks[1][:].rearrange("p (h n) -> p h n", h=4)
            OE = banks[2][:].rearrange("p (h n) -> p h n", h=4)
            DC = banks[3][:].rearrange("p (h n) -> p h n", h=4)
            SO = banks[1][:].rearrange("p (h n) -> p h n", h=8)
            KT = banks[0][:, 0:256].bitcast(BF16).rearrange("p (h n) -> p h n", h=4)
            RTq = banks[0][:, 256:512].bitcast(BF16).rearrange("p (h n) -> p h n", h=4)

            x1_bufs = []

            def load_x_tile(b, ti):
                for j in range(4):
                    slot = ((b * TPS + ti) * 4 + j) % 8
                    t0 = b * S + ti * 512 + j * 128
                    k.dma("sp", XB[:, slot, :], x_d[t0:t0 + 128, :], (), [b_XB[slot]])

            tiles = [(b, ti) for b in range(NSEQ) for ti in range(TPS)]
            load_x_tile(*tiles[0])

            for tix, (b, ti) in enumerate(tiles):
                if tix + 1 < len(tiles):
                    load_x_tile(*tiles[tix + 1])
                if ti == 0:
                    k.dma("sp", GP1[:], gsc[0, b:b + 1, :].partition_broadcast(128), b_gsc[0], [b_GP1])
                    k.memset("pool", CT[:], 0.0, [b_CT])
                    k.memset("pool", CTb[:], 0.0, [b_CTb])
                    k.memset("pool", nv[:], 0.0, [b_nv])
                    k.memset("pool", nvb[:], 0.0, [b_nvb])
                    k.memset("pool", halo[:], 0.0, [b_halo])
                for j in range(4):
                    slot = (tix * 4 + j) % 8
                    first_dep = (b_gsc[0] + b_gsc[1]) if (tix == 0 and j == 0) else []
                    k.act(junk[:], XB[:, slot, :], AF.Square, [b_XB[slot]] + first_dep, [b_junk, b_st], accum=st[:, j:j + 1])
                k.ts("dve", st[:, 4:8], st[:, 0:4], 1.0 / D, EPS, ALU.mult, ALU.add, [b_st], [b_st])
                k.tt("pool", st[:, 8:12], st[:, 4:8], neghalf[:, 0:4], ALU.pow, [b_st] + RC, [b_st])
                TR7 = banks[7][:].bitcast(BF16).rearrange("p (k n) -> p k n", k=8)
                def emit_xn(j):
                    slot = (tix * 4 + j) % 8
                    k.act(xn[:, j % 2, :], XB[:, slot, :], AF.Identity, [b_XB[slot], b_st], [b_xn[j % 2]],
                          scale=st[:, 8 + j:9 + j])

                emit_xn(0)
                for j in range(4):
                    a_ = j % 2
                    TRj, bTR = (TR, [b_bk[0]]) if a_ == 0 else (TR7, [b_bk[7]])
                    for kk in range(8):
                        k.tr(TRj[:, kk, :], xn[:, a_, kk * 128:(kk + 1) * 128], identb[:], [b_xn[a_]] + RC, bTR)
                    if j + 1 < 4:
                        emit_xn(j + 1)
                    for kk in range(8):
                        k.ts("dve", hT[:, kk, j * 128:(j + 1) * 128], TRj[:, kk, :], G1[:, kk, b:b + 1],
                             SH1[:, kk, b:b + 1], ALU.mult, ALU.add, bTR + [b_G], [b_hT[j]])
                for j in range(4):
                    for kk in range(8):
                        k.mm(GT[:, j, :], hT[:, kk, j * 128:(j + 1) * 128], Win[:, kk, 2048:2056], kk == 0, kk == 7,
                             [b_hT[j], b_Win[kk]], [b_GT])
                g3 = lambda i: gat[:, i, :].rearrange("p (j h) -> p j h", h=4)
                k.tt("dve", g3(0), GT[:, :, 4:8], fb16[:].rearrange("p (j h) -> p j h", h=4), ALU.add, [b_GT] + RC, [b_gat])
                k.tt("dve", g3(1), GT[:, :, 0:4], ib16[:].rearrange("p (j h) -> p j h", h=4), ALU.add, [b_GT] + RC, [b_gat])
                k.act(gat[:, 2, :], gat[:, 0, :], AF.Exp, [b_gat], [b_gat], scale=-1.0)
                k.act(gat[:, 3, :], gat[:, 2, :], AF.Ln, [b_gat], [b_gat], bias=1.0)
                for c in range(8):
                    s = c % 2
                    bq = (b_bk1, b_bk2)[s]
                    for kk in range(8):
                        k.mm(QK[s], Win[:, kk, c * 128:(c + 1) * 128], hT[:, kk, :], kk == 0, kk == 7,
                             b_hT + [b_Win[kk]], [bq])
                    k.cp("act", U[:, s, 3:515], QK[s], [bq], [b_U[s]])
                    k.cp("act", U[:, s, 0:3], halo[:, c, :], [b_halo], [b_U[s]])
                    k.cp("act", halo[:, c, :], QK[s][:, 509:512], [bq], [b_halo])
                    k.act(acc[:, s, :], QK[s], AF.Identity, [bq] + RC, [b_acc[s]], scale=mlcw[:, c, 3:4], bias=mlcb[:, c:c + 1])
                    for tap in (2, 1, 0):
                        k.stt(acc[:, s, :], U[:, s, tap:tap + 512], mlcw[:, c, tap:tap + 1], acc[:, s, :],
                              ALU.mult, ALU.add, [b_U[s], b_acc[s]] + RC, [b_acc[s]])
                    k.act(qkT[:, c, :], acc[:, s, :], AF.Silu, [b_acc[s]], [b_qkT[c]])
                k.mm(CSa, triu[:], gat[:, 3, :], True, True, [b_gat] + RC, [b_CS])
                k.mm(CSb, onesf[:], gat[:, 3, :], True, True, [b_gat] + RC, [b_CS])
                k.tt("dve", gat[:, 4, :], gat[:, 1, :], CSa, ALU.add, [b_gat, b_CS], [b_gat])
                k.act(gat[:, 5, :], gat[:, 4, :], AF.Exp, [b_gat], [b_gat])
                k.act(gat[:, 6, :], CSa, AF.Exp, [b_CS], [b_gat])
                k.act(gat[:, 7, :], CSb, AF.Exp, [b_CS], [b_gat], scale=-1.0)
                k.ts("dve", sd[:], gat[:, 7, :], KSCALE, None, ALU.mult, None, [b_gat], [b_gat])
                k.cp("dve", eab[:], gat[:, 5, :], [b_gat], [b_gat])
                ea, eb, dec = gat[:, 5, :], gat[:, 6, :], gat[:, 7, :]

                for j in range(4):
                    blk = ti * 4 + j
                    cols = slice(j * 128, (j + 1) * 128)
                    vs = blk % 8
                    for h in range(2):
                        outp, bb = (A0, b_bk3) if h == 0 else (A1, b_bk4)
                        for kk in range(8):
                            k.mm(outp, hT[:, kk, cols], Win[:, kk, 1024 + h * 512: 1536 + h * 512], kk == 0, kk == 7,
                                 [b_hT[j], b_Win[kk]], [bb])
                    for kk in range(8):
                        k.mm(Bt0, hT[:, kk, cols], Win[:, kk, 2056:2568], kk == 0, kk == 7, [b_hT[j], b_Win[kk]], [b_bk5])
                    for kk in range(8):
                        k.mm(Bt1, hT[:, kk, cols], Win[:, kk, 2568:2824], kk == 0, kk == 7, [b_hT[j], b_Win[kk]], [b_bt1])
                    for h in range(4):
                        k.act(vtil[:, j, h, :], A0[:, h * 128:(h + 1) * 128], AF.Identity, [b_bk3, b_gat], [b_vtil[j]],
                              scale=ea[:, j * 4 + h: j * 4 + h + 1])
                    k.act(tanho[:], A1, AF.Tanh, [b_bk4], [b_tanho], scale=0.5)
                    k.stt(gso[:, j, :], tanho[:], 1.0, mlgh[:], ALU.add, ALU.mult, [b_tanho] + RC, [b_gso[j]])
                    X4 = Bt0[:, 0:512].rearrange("p (h t d) -> p h t d", t=2, d=32)
                    Xk = Bt1[:, 0:128].rearrange("p (h t d) -> p h t d", t=2, d=32)
                    rotj = rot[:, j, :]
                    for (X, nh, off) in ((X4, 8, 0), (Xk, 2, 512)):
                        bsrc = b_bk5 if nh == 8 else b_bt1
                        t1v = rt1[:, off:off + nh * 64].rearrange("p (h t d) -> p h t d", t=2, d=32)
                        t2v = rt2[:, off:off + nh * 64].rearrange("p (h t d) -> p h t d", t=2, d=32)
                        rov = rotj[:, off:off + nh * 64].rearrange("p (h t d) -> p h t d", t=2, d=32)
                        cb = cosT[:, blk, :].unsqueeze(1).unsqueeze(1).broadcast_to([128, nh, 2, 32])
                        sb_ = sinT[:, blk, :].unsqueeze(1).broadcast_to([128, nh, 32])
                        k.tt("dve", t1v, X, cb, ALU.mult, [bsrc] + RC, [b_rt1])
                        k.tt("dve", t2v[:, :, 0, :], X[:, :, 1, :], sb_, ALU.mult, [bsrc] + RC, [b_rt2])
                        k.tt("dve", t2v[:, :, 1, :], X[:, :, 0, :], sb_, ALU.mult, [bsrc] + RC, [b_rt2])
                        if nh == 8:
                            pv = lambda t, i: t[:, off:off + 512].rearrange("p (g r t d) -> p g r t d", g=2, r=4, t=2)[:, :, :, i, :]
                            po = lambda i: rotj[:, 0:512].rearrange("p (r g t d) -> p g r t d", g=2, r=4, t=2)[:, :, :, i, :]
                            k.tt("pool", po(0), pv(rt1, 0), pv(rt2, 0), ALU.subtract, [b_rt1, b_rt2], [b_rot[j]])
                            k.tt("pool", po(1), pv(rt1, 1), pv(rt2, 1), ALU.add, [b_rt1, b_rt2], [b_rot[j]])
                        else:
                            k.tt("pool", rov[:, :, 0, :], t1v[:, :, 0, :], t2v[:, :, 0, :], ALU.subtract, [b_rt1, b_rt2], [b_rot[j]])
                            k.tt("pool", rov[:, :, 1, :], t1v[:, :, 1, :], t2v[:, :, 1, :], ALU.add, [b_rt1, b_rt2], [b_rot[j]])
                    k.cp("act", vat[:, vs, :], Bt1[:, 128:256], [b_bt1], [b_vat[vs]])
                for j in range(4):
                    blk = ti * 4 + j
                    cols = slice(j * 128, (j + 1) * 128)
                    vs = blk % 8
                    if j % 2 == 0:
                        STv, bST = banks[1][:].rearrange("p (h n) -> p h n", h=4), [b_bk1]
                        KTv, bKT = KT, [b_bk0a]
                        RTv, bRT = RTq, [b_bk0b]
                    else:
                        STv, bST = banks[2][:].rearrange("p (h n) -> p h n", h=4), [b_bk2]
                        KTv = banks[7][:, 0:256].bitcast(BF16).rearrange("p (h n) -> p h n", h=4)
                        RTv = banks[7][:, 256:512].bitcast(BF16).rearrange("p (h n) -> p h n", h=4)
                        bKT = bRT = [b_bk7]
                    for h in range(4):
                        k.mm(STv[:, h, :], qkT[:, 4 + h, cols], qkT[:, h, cols], True, True, [b_qkT[4 + h], b_qkT[h]], bST)
                    k.tt("dve", STm[:, j, :, :], STv, masku[:].unsqueeze(1).broadcast_to([128, 4, 128]), ALU.mult,
                         bST + RC, [b_STm[j]])
                    for h in range(4):
                        k.tr(KTv[:, h, :], qkT[:, 4 + h, cols], identb[:], [b_qkT[4 + h]] + RC, bKT)
                    k.cp("act", ktok[:, j, :, :], KTv, bKT, [b_ktok[j]])
                    for pr in range(4):
                        k.tr(RTv[:, pr, :], rot[:, j, pr * 128:(pr + 1) * 128], identb[:], [b_rot[j]] + RC, bRT)
                    k.tr(RTk, rot[:, j, 512:640], identb[:], [b_rot[j]] + RC, [b_RTk])
                    k.cp("act", qTa[:, j, :, :], RTv, bRT, [b_qTa[j]])
                    k.cp("act", kTr[:, vs, :], RTk, [b_RTk], [b_kTr[vs]])

                def o_transposes(j):
                    a = j % 2
                    for kk in range(8):
                        k.tr(TR[:, kk, :], mix[:, a, kk * 128:(kk + 1) * 128], identb[:], [b_mixm[a], b_mixa[a]] + RC,
                             [b_bk0a, b_bk0b])
                    k.cp("act", mixT[:, a, :, :], TR, [b_bk0a, b_bk0b], [b_mixT[a]])

                def o_rest(j):
                    a = j % 2
                    blk = ti * 4 + j
                    slot = (tix * 4 + j) % 8
                    for h in range(2):
                        outp, bb = (A1, b_bk4) if h == 0 else (Bt0, b_bk5)
                        for kk in range(8):
                            k.mm(outp, mixT[:, a, kk, :], Wout[:, kk, h * 512:(h + 1) * 512], kk == 0, kk == 7,
                                 [b_mixT[a], b_Wout[kk]], [bb])
                    ys = yst[:, a, :]
                    k.act(junk[:, 0:512], A1, AF.Square, [b_bk4], [b_junk, b_yst[a]], accum=ys[:, 0:1])
                    k.act(junk[:, 512:1024], Bt0, AF.Square, [b_bk5], [b_junk, b_yst[a]], accum=ys[:, 1:2])
                    k.tt("dve", ys[:, 2:3], ys[:, 0:1], ys[:, 1:2], ALU.add, [b_yst[a]], [b_yst[a]])
                    k.ts("dve", ys[:, 3:4], ys[:, 2:3], 1.0 / D, EPS, ALU.mult, ALU.add, [b_yst[a]], [b_yst[a]])
                    k.tt("pool", ys[:, 4:5], ys[:, 3:4], neghalf[:, 0:1], ALU.pow, [b_yst[a]] + RC, [b_yst[a]])
                    xb = XB[:, slot, :]
                    for h, (outp, bb) in enumerate(((A1, b_bk4), (Bt0, b_bk5))):
                        k.stt(outp, outp, ys[:, 4:5], GP1[:, 0, h * 512:(h + 1) * 512], ALU.mult, ALU.mult,
                              [bb, b_yst[a], b_GP1], [bb])
                        k.tt("dve", xb[:, h * 512:(h + 1) * 512], outp, xb[:, h * 512:(h + 1) * 512], ALU.add,
                             [bb, b_XB[slot]], [b_XB[slot]])
                    t0 = b * S + blk * 128
                    bx1 = B(f"x1_{t0}")
                    x1_bufs.append(bx1)
                    k.dma("sp", x1s[t0:t0 + 128, :], xb, [b_XB[slot]], [bx1])

                for j in range(4):
                    blk = ti * 4 + j
                    cols = slice(j * 128, (j + 1) * 128)
                    vs, pvs = blk % 8, (blk - 1) % 8
                    a = j % 2
                    kbs = ([(pvs, maskp)] if blk > 0 else []) + [(vs, maskc)]
                    scl = [(g, kb, msk) for g in range(2) for (kb, msk) in kbs]
                    pidx = {(g, kb): i for i, (g, kb, _) in enumerate(scl)}

                    def emit_sc(i):
                        if i >= len(scl):
                            return
                        g, kb, msk = scl[i]
                        k.mm(SC, kTr[g * 64:(g + 1) * 64, kb, :],
                             qTa[g * 64:(g + 1) * 64, j, :, :].rearrange("p h n -> p (h n)"), True, False,
                             [b_kTr[kb], b_qTa[j]], [b_bk7])
                        k.mm(SC, identb[:], msk[:], False, True, RC, [b_bk7])
                        k.act(pT[:, i, :], SC, AF.Exp, [b_bk7], [b_pT[i]], scale=0.125)

                    emit_sc(0)
                    for h in range(4):
                        k.mm(DC[:, h, :], ktok[:, j, h, :], vtil[:, j, h, :], True, True, [b_ktok[j], b_vtil[j]], [b_bk3])
                        k.mm(DNs[:, h:h + 1], ktok[:, j, h, :], eab[:, j * 4 + h: j * 4 + h + 1], True, True,
                             [b_ktok[j], b_gat], [b_DNs])
                    emit_sc(1)
                    for h in range(4):
                        k.mm(OE[:, h, :], qkT[:, h, cols], CTb[:, h, :], True, False, [b_qkT[h], b_CTb], [b_bk2])
                        k.mm(OE[:, h, :], STm[:, j, h, :], vtil[:, j, h, :], False, True, [b_STm[j], b_vtil[j]], [b_bk2])
                    for h in range(4):
                        k.mm(DN[:, h:h + 1], qkT[:, h, cols], nvb[:, h:h + 1], True, False, [b_qkT[h], b_nvb], [b_DN])
                        k.mm(DN[:, h:h + 1], STm[:, j, h, :], eab[:, j * 4 + h: j * 4 + h + 1], False, True,
                             [b_STm[j], b_gat], [b_DN])
                    for h in range(4):
                        k.ts("dve", CT[:, h, :], CT[:, h, :], dec[:, j * 4 + h: j * 4 + h + 1], None, ALU.mult, None,
                             [b_CT, b_gat], [b_CT])
                        k.stt(CT[:, h, :], DC[:, h, :], sd[:, j * 4 + h: j * 4 + h + 1], CT[:, h, :], ALU.mult, ALU.add,
                              [b_bk3, b_gat, b_CT], [b_CT])
                    k.tt("dve", nv[:], nv[:], dec[:, j * 4:(j + 1) * 4], ALU.mult, [b_nv, b_gat], [b_nv])
                    k.tt("dve", nvt[:], DNs, sd[:, j * 4:(j + 1) * 4], ALU.mult, [b_DNs, b_gat], [b_nv])
                    k.tt("dve", nv[:], nv[:], nvt[:], ALU.add, [b_nv], [b_nv])
                    ms_ = mst[:, a, :, :]
                    bm = b_mst[a]
                    k.act(ms_[:, 0, :], DN, AF.Abs, [b_DN], [bm])
                    k.tt("dve", ms_[:, 1, :], ms_[:, 0, :], eb[:, j * 4:(j + 1) * 4], ALU.max, [bm, b_gat], [bm])
                    k.tt("dve", ms_[:, 2, :], ms_[:, 1, :], ms_[:, 1, :], ALU.mult, [bm], [bm])
                    for h in range(4):
                        k.act(junk[:, h * 128:(h + 1) * 128], OE[:, h, :], AF.Square, [b_bk2], [b_junk, bm],
                              accum=ms_[:, 3, h:h + 1])
                    k.ts("dve", ms_[:, 4, :], ms_[:, 3, :], 1.0 / 128.0, None, ALU.mult, None, [bm], [bm])
                    k.stt(ms_[:, 5, :], ms_[:, 2, :], EPS, ms_[:, 4, :], ALU.mult, ALU.add, [bm], [bm])
                    k.tt("pool", ms_[:, 6, :], ms_[:, 5, :], neghalf[:, 0:4], ALU.pow, [bm] + RC, [bm])
                    k.cp("act", CTb[:], CT[:], [b_CT], [b_CTb])
                    k.cp("dve", nvb[:], nv[:], [b_nv], [b_nvb])
                    for h in range(4):
                        k.stt(mix[:, a, h * 128:(h + 1) * 128], OE[:, h, :], ms_[:, 6, h:h + 1],
                              gso[:, j, h * 128:(h + 1) * 128], ALU.mult, ALU.mult, [b_bk2, bm, b_gso[j]], [b_mixm[a]])
                    emit_sc(2)
                    if j > 0:
                        o_transposes(j - 1)
                    emit_sc(3)
                    if j > 0:
                        o_rest(j - 1)
                    for g in range(2):
                        for r in range(4):
                            hh = g * 4 + r
                            for ii, (kb, _) in enumerate(kbs):
                                pi = pidx[(g, kb)]
                                k.mm(SO[:, hh, :], pT[:, pi, r * 128:(r + 1) * 128], vat[:, kb, g * 64:(g + 1) * 64],
                                     ii == 0, ii == len(kbs) - 1, [b_pT[pi], b_vat[kb]], [b_bk1])
                            for ii, (kb, _) in enumerate(kbs):
                                pi = pidx[(g, kb)]
                                k.mm(SD[:, hh:hh + 1], pT[:, pi, r * 128:(r + 1) * 128], onesb[:, 0:1],
                                     ii == 0, ii == len(kbs) - 1, [b_pT[pi]] + RC, [b_SD])
                    as_ = ast[:, a, :]
                    ba = b_ast[a]
                    ob = osb[:, 0, :]
                    k.tt("dve", as_[:, 0:8], SD, esink[:], ALU.add, [b_SD] + RC, [ba])
                    k.recip(as_[:, 8:16], as_[:, 0:8], [ba], [ba])
                    k.tt("dve", ob.rearrange("p (h d) -> p h d", d=64), SO,
                         as_[:, 8:16].unsqueeze(2).broadcast_to([128, 8, 64]), ALU.mult, [b_bk1, ba], [b_osb[a]])
                    k.act(junk[:, 512:1024], ob, AF.Square, [b_osb[a]], [b_junk, ba], accum=as_[:, 16:17])
                    k.ts("dve", as_[:, 17:18], as_[:, 16:17], 1.0 / 512.0, EPS, ALU.mult, ALU.add, [ba], [ba])
                    k.tt("pool", as_[:, 18:19], as_[:, 17:18], neghalf[:, 0:1], ALU.pow, [ba] + RC, [ba])
                    k.stt(mix[:, a, 512:1024], ob, as_[:, 18:19], attg[:], ALU.mult, ALU.mult, [b_osb[a], ba] + RC, [b_mixa[a]])
                o_transposes(3)
                o_rest(3)

            k.fence("sp", x1_bufs + b_gsc[0] + b_gsc[1] + [b_G])
            with ExitStack() as sems, nc.Block() as block:
                P.emit(block, sems)
            stats1 = P.stats

        with ExitStack() as ph:
            def sb2(name, shape, dt=F32):
                return ph.enter_context(nc.sbuf_tensor("t_" + name, list(shape), dt))

            P = Prog(nc)
            k = K(P)
            B = lambda n: Buf(n)
            Wup = sb2("Wup", [128, 8, 2 * DFF], BF16)
            Wdn = sb2("Wdn", [128, NCH, D], BF16)
            NCG = (NCH + 3) // 4
            b_Wup = [[B(f"Wup{h}_{g}") for g in range(NCG)] for h in range(2)]
            b_Wdn = [B(f"Wdn{i}") for i in range(NCH)]
            def load_wup(cg):
                c_lo, c_hi = cg * 4, min(NCH, cg * 4 + 4)
                w_ = (c_hi - c_lo) * 128
                for h in range(2):
                    col0 = h * DFF + c_lo * 128
                    k.dma("pool", Wup[:, :, col0:col0 + w_],
                          wup_d[:, col0:col0 + w_].rearrange("(k p) n -> p k n", p=128), (), [b_Wup[h][cg]])

            def load_rest_weights():
                for cg in range(1, NCG):
                    load_wup(cg)

            def load_wdn(i):
                k.dma("pool", Wdn[:, i, :], wdn_d[i * 128:(i + 1) * 128, :], (), [b_Wdn[i]])

            load_wup(0)
            RC = [B("c_ffw"), B("c_ffb")]
            ffw = sb2("ffw", [128, 2 * NCH, 3])
            ffb = sb2("ffb", [128, 2 * NCH])
            k.dma("sp", ffw[:], ffw_d, (), [RC[0]])
            k.dma("sp", ffb[:], ffb_d, (), [RC[1]])
            XR = sb2("XR", [128, 2, D])
            b_XR = [B("XR0"), B("XR1")]
            XO = sb2("XO", [128, 2, D])
            b_XO = [B("XO0"), B("XO1")]
            GP2 = sb2("GP2", [128, 1, D])
            b_GP2 = B("GP2")
            hT2 = sb2("hT2", [128, 8, 512], BF16)
            b_hT2 = [B(f"hT2{j}") for j in range(4)]
            aT = sb2("aT", [128, NCH, 512], BF16)
            b_aT = [B(f"aT{c}") for c in range(NCH)]
            xn2 = sb2("xn2", [128, D], BF16)
            b_xn2 = B("xn2")
            junk2 = sb2("junk2", [128, D], BF16)
            b_junk2 = B("junk2")
            Ug = sb2("Ug", [128, 514])
            Uv = sb2("Uv", [128, 514])
            accg = sb2("accg", [128, 512])
            accv = sb2("accv", [128, 512])
            gl = sb2("gl", [128, 512])
            b_Ug, b_Uv, b_accg, b_accv, b_gl = B("Ug"), B("Uv"), B("accg"), B("accv"), B("gl")
            halo2 = sb2("halo2", [128, 2 * NCH, 2])
            b_halo2 = B("halo2")
            st2 = sb2("st2", [128, 16])
            b_st2 = B("st2")
            yst2 = sb2("yst2", [128, 8])
            b_yst2 = B("yst2")

            TR2 = banks[0][:].bitcast(BF16).rearrange("p (k n) -> p k n", k=8)
            b_tr2 = Buf("tr2", excl=True)
            PG = [banks[1][:], banks[3][:]]
            PV = [banks[2][:], banks[4][:]]
            b_PG = [Buf("PG0", excl=True), Buf("PG1", excl=True)]
            b_PV = [Buf("PV0", excl=True), Buf("PV1", excl=True)]
            Y2 = [banks[5][:], banks[6][:]]
            b_Y2 = [Buf("Y20", excl=True), Buf("Y21", excl=True)]

            out_bufs = []
            tiles = [(b, ti) for b in range(NSEQ) for ti in range(TPS)]

            xr_loaded = set()
            xo_loaded = set()

            def load_xr(tix, j):
                b_, ti_ = tiles[tix]
                t0_ = b_ * S + ti_ * 512 + j * 128
                s_ = (tix * 4 + j) % 2
                k.dma("sp", XR[:, s_, :], x1s[t0_:t0_ + 128, :], (), [b_XR[s_]])
                xr_loaded.add((tix, j))

            def load_xo(tix, j):
                b_, ti_ = tiles[tix]
                t0_ = b_ * S + ti_ * 512 + j * 128
                s_ = (tix * 4 + j) % 2
                k.dma("sp", XO[:, s_, :], x1s[t0_:t0_ + 128, :], (), [b_XO[s_]])
                xo_loaded.add((tix, j))

            def norm_a(tix, j):
                b, ti = tiles[tix]
                t0 = b * S + ti * 512 + j * 128
                s = (tix * 4 + j) % 2
                if (tix, j) not in xr_loaded:
                    load_xr(tix, j)
                xr = XR[:, s, :]
                k.act(junk2[:], xr, AF.Square, [b_XR[s]], [b_junk2, b_st2], accum=st2[:, 0:1])
                k.ts("dve", st2[:, 1:2], st2[:, 0:1], 1.0 / D, EPS, ALU.mult, ALU.add, [b_st2], [b_st2])
                k.tt("pool", st2[:, 2:3], st2[:, 1:2], neghalf[:, 0:1], ALU.pow, [b_st2], [b_st2])
                k.act(xn2[:], xr, AF.Identity, [b_XR[s], b_st2], [b_xn2], scale=st2[:, 2:3])

            def norm_b(tix, j):
                b, ti = tiles[tix]
                for kk in range(8):
                    k.tr(TR2[:, kk, :], xn2[:, kk * 128:(kk + 1) * 128], identb[:], [b_xn2], [b_tr2])
                for kk in range(8):
                    k.ts("dve", hT2[:, kk, j * 128:(j + 1) * 128], TR2[:, kk, :], G2[:, kk, b:b + 1], SH2[:, kk, b:b + 1],
                         ALU.mult, ALU.add, [b_tr2], [b_hT2[j]])

            def norm_sub(tix, j):
                norm_a(tix, j)
                norm_b(tix, j)

            b_bank7 = Buf("bk7_2", excl=True)
            Y2sets = [((banks[7][:], banks[1][:]), (b_bank7, b_PG[0])), ((Y2[0], Y2[1]), (b_Y2[0], b_Y2[1]))]
            yst2d = sb2("yst2d", [128, 2, 8])
            b_yst2d = [B("yst2d0"), B("yst2d1")]

            for j in range(4):
                norm_sub(0, j)
            load_rest_weights()
            for tix, (b, ti) in enumerate(tiles):
                if ti == 0:
                    k.dma("sp", GP2[:], gsc[1, b:b + 1, :].partition_broadcast(128), (), [b_GP2])
                    k.memset("pool", halo2[:], 0.0, [b_halo2])
                for c in range(NCH):
                    s = c % 2
                    for kk in range(8):
                        k.mm(PG[s], Wup[:, kk, c * 128:(c + 1) * 128], hT2[:, kk, :], kk == 0, kk == 7,
                             b_hT2 + [b_Wup[0][c // 4]], [b_PG[s]])
                    for kk in range(8):
                        k.mm(PV[s], Wup[:, kk, DFF + c * 128: DFF + (c + 1) * 128], hT2[:, kk, :], kk == 0, kk == 7,
                             b_hT2 + [b_Wup[1][c // 4]], [b_PV[s]])
                    for (Ux, bU, ps, bps, ac, bac, ci) in ((Ug, b_Ug, PG[s], b_PG[s], accg, b_accg, c),
                                                           (Uv, b_Uv, PV[s], b_PV[s], accv, b_accv, NCH + c)):
                        k.cp("act", Ux[:, 2:514], ps, [bps], [bU])
                        k.cp("pool", Ux[:, 0:2], halo2[:, ci, :], [b_halo2], [bU])
                        k.cp("pool", halo2[:, ci, :], Ux[:, 512:514], [bU], [b_halo2])
                        k.act(ac[:], ps, AF.Identity, [bps] + RC, [bac], scale=ffw[:, ci, 2:3], bias=ffb[:, ci:ci + 1])
                        for tap in (1, 0):
                            k.stt(ac[:], Ux[:, tap:tap + 512], ffw[:, ci, tap:tap + 1], ac[:], ALU.mult, ALU.add,
                                  [bU, bac] + RC, [bac])
                    if tix == 0 and c < NCH // 2:
                        load_wdn(2 * c)
                        load_wdn(2 * c + 1)
                    k.act(gl[:], accg[:], AF.Gelu_apprx_tanh, [b_accg], [b_gl])
                    k.tt("dve", aT[:, c, :], gl[:], accv[:], ALU.mult, [b_gl, b_accv], [b_aT[c]])
                for j in range(4):
                    t0 = b * S + ti * 512 + j * 128
                    s = (tix * 4 + j) % 2
                    (Ya, Yb), (bYa, bYb) = Y2sets[j % 2]
                    Yh, bYh = (Ya, Yb), (bYa, bYb)
                    ys, bys = yst2d[:, j % 2, :], b_yst2d[j % 2]
                    if (tix, j) not in xo_loaded:
                        load_xo(tix, j)
                    if j + 1 < 4:
                        load_xo(tix, j + 1)
                        if tix + 1 < len(tiles):
                            load_xr(tix + 1, j + 1)
                    if tix + 1 < len(tiles):
                        norm_a(tix + 1, j)
                    for h in range(2):
                        for c in range(NCH):
                            k.mm(Yh[h], aT[:, c, j * 128:(j + 1) * 128], Wdn[:, c, h * 512:(h + 1) * 512], c == 0, c == NCH - 1,
                                 [b_aT[c], b_Wdn[c]], [bYh[h]])
                    if tix + 1 < len(tiles):
                        norm_b(tix + 1, j)
                    for h in range(2):
                        k.act(junk2[:, h * 512:(h + 1) * 512], Yh[h], AF.Square, [bYh[h]], [b_junk2, bys],
                              accum=ys[:, h:h + 1])
                    k.tt("dve", ys[:, 2:3], ys[:, 0:1], ys[:, 1:2], ALU.add, [bys], [bys])
                    k.ts("dve", ys[:, 3:4], ys[:, 2:3], 1.0 / D, EPS, ALU.mult, ALU.add, [bys], [bys])
                    k.tt("pool", ys[:, 4:5], ys[:, 3:4], neghalf[:, 0:1], ALU.pow, [bys], [bys])
                    xo = XO[:, s, :]
                    for h in range(2):
                        k.stt(Yh[h], Yh[h], ys[:, 4:5], GP2[:, 0, h * 512:(h + 1) * 512], ALU.mult, ALU.mult,
                              [bYh[h], bys, b_GP2], [bYh[h]])
                        k.tt("dve", xo[:, h * 512:(h + 1) * 512], Yh[h], xo[:, h * 512:(h + 1) * 512], ALU.add,
                             [bYh[h], b_XO[s]], [b_XO[s]])
                    bo = B(f"o{t0}")
                    out_bufs.append(bo)
                    k.dma("sp", out_d[t0:t0 + 128, :], xo, [b_XO[s]], [bo])
            k.fence("sp", out_bufs)
            with ExitStack() as sems, nc.Block() as block:
                P.emit(block, sems)
            stats2 = P.stats
    nc._stats = (stats1, stats2)
    return nc


def _consts(NSEQ, S):
    NB = S // 128
    bf = ml_dtypes.bfloat16
    p = np.arange(128)
    identb = np.eye(128, dtype=np.float32).astype(bf)
    triu = (p[:, None] <= p[None, :]).astype(np.float32)
    masku = triu * np.float32(KSCALE)
    mc = np.where(p[:, None] <= p[None, :], 0.0, NEG).astype(np.float32)
    mp = np.where(p[:, None] > p[None, :], 0.0, NEG).astype(np.float32)
    maskc = np.tile(mc, (1, 4)).astype(bf)
    maskp = np.tile(mp, (1, 4)).astype(bf)
    half = 32
    inv = (10000.0 ** (-np.arange(half, dtype=np.float32) / half)).astype(np.float32)
    pos = (np.arange(NB)[None, :] * 128 + p[:, None]).astype(np.float32)
    ang = pos[:, :, None] * inv[None, None, :]
    return dict(identb=identb, triu=triu, masku=masku, maskc=maskc, maskp=maskp,
                cos=np.cos(ang).astype(np.float32), sin=np.sin(ang).astype(np.float32),
                i4=np.eye(4, dtype=np.float32))


def _fp(v, nk):
    return np.ascontiguousarray(np.asarray(v, np.float32).reshape(nk, 128).T)


def _rb(v):
    v = np.asarray(v, np.float32).reshape(1, -1)
    return np.ascontiguousarray(np.broadcast_to(v, (128, v.shape[1])))


_NC_CACHE = {}


def run(inputs, NSEQ, S):
    f = lambda a: np.ascontiguousarray(np.asarray(a, np.float32))
    x = f(inputs["x"])
    c = f(inputs["c"])
    assert x.shape[0] == NCORES * NSEQ and x.shape[1] == S
    key = (NSEQ, S)
    if key not in _NC_CACHE:
        _NC_CACHE[key] = build(NSEQ, S)
    nc = _NC_CACHE[key]
    cst = _consts(NSEQ, S)
    shared = dict(cst)
    shared["w_ada"] = f(inputs["w_ada"][0])
    shared["b_ada"] = f(inputs["b_ada"][0]).reshape(1, -1)
    shared["w_in"] = f(inputs["w_in"][0])
    shared["w_out"] = f(inputs["w_out"][0])
    shared["w_up"] = f(inputs["w_up"][0])
    shared["w_down"] = f(inputs["w_down"][0])
    shared["pmg"] = _fp(inputs["pre_mix_g"][0], 8)
    shared["pfg"] = _fp(inputs["pre_ffn_g"][0], 8)
    post = np.stack([f(inputs["post_mix_g"][0]), f(inputs["post_ffn_g"][0])], 0)
    shared["post4"] = np.ascontiguousarray(np.broadcast_to(post[None], (NSEQ, 2, D)))
    shared["mlcw"] = np.ascontiguousarray(f(inputs["ml_conv_w"][0]).reshape(4, 8, 128).transpose(2, 1, 0))
    shared["mlcb"] = _fp(inputs["ml_conv_b"][0], 8)
    shared["ib16"] = _rb(np.tile(f(inputs["ml_i_b"][0]), 4))
    shared["fb16"] = _rb(np.tile(f(inputs["ml_f_b"][0]), 4))
    shared["mlg"] = _rb(f(inputs["ml_norm_g"][0]).reshape(-1))
    shared["sinks"] = _rb(inputs["attn_sinks"][0])
    shared["attg"] = _rb(inputs["attn_norm_g"][0])
    shared["ffw"] = np.ascontiguousarray(f(inputs["ffn_conv_w"][0]).reshape(3, 2 * NCH, 128).transpose(2, 1, 0))
    shared["ffb"] = _fp(inputs["ffn_conv_b"][0], 2 * NCH)
    in_maps = []
    for i in range(NCORES):
        m = dict(shared)
        m["x"] = np.ascontiguousarray(x[i * NSEQ:(i + 1) * NSEQ].reshape(NSEQ * S, D))
        ci = c[i * NSEQ:(i + 1) * NSEQ]
        m["cT"] = np.ascontiguousarray(ci.reshape(NSEQ, 8, 128).transpose(2, 1, 0))
        in_maps.append(m)
    res = run_bass_kernel_spmd(nc, in_maps, core_ids=list(range(NCORES)))
    outs = [np.asarray(r["out"], np.float32).reshape(NSEQ, S, D) for r in res.results]
    return np.concatenate(outs, axis=0)


def kernel(**inputs):
    x = inputs["x"]
    return run(inputs, x.shape[0] // NCORES, x.shape[1])
```

```python
import math
from contextlib import ExitStack

import numpy as np
import ml_dtypes

import concourse.bass as bass
import concourse.mybir as mybir
from concourse.bass_utils import run_bass_kernel_spmd

F32 = mybir.dt.float32
BF16 = mybir.dt.bfloat16
AF = mybir.ActivationFunctionType
ALU = mybir.AluOpType

NCORES = 8
D = 1024
DFF = 2816
NCH = DFF // 128
INC = 2824
EPS = 1e-6
KSCALE = 1.0 / math.sqrt(128.0)
NEG = -30000.0

SEM_LIMIT = 24000
ENGS = ("pe", "act", "dve", "pool", "sp")


class Buf:
    __slots__ = ("name", "last_w", "readers", "excl")

    def __init__(self, name, excl=False):
        self.name = name
        self.last_w = None
        self.readers = []
        self.excl = excl


class Op:
    __slots__ = ("eng", "fn", "deps", "is_dma", "needed", "waits", "tok")

    def __init__(self, eng, fn, deps, is_dma):
        self.eng = eng
        self.fn = fn
        self.deps = deps
        self.is_dma = is_dma
        self.needed = False
        self.waits = []
        self.tok = None


class Prog:
    def __init__(self, nc, n_dma_sems=12):
        self.nc = nc
        self.ops = []
        self.n_dma_sems = n_dma_sems

    def op(self, eng, fn, reads=(), writes=(), dma=False):
        ex = [b for b in reads if b.excl]
        if ex:
            writes = list(writes) + [b for b in ex if b not in writes]
            reads = [b for b in reads if not b.excl]
        deps = set()
        for b in reads:
            if b.last_w is not None:
                deps.add(b.last_w)
        for b in writes:
            if b.last_w is not None:
                deps.add(b.last_w)
            deps.update(b.readers)
        oid = len(self.ops)
        self.ops.append(Op(eng, fn, deps, dma))
        for b in writes:
            b.last_w = oid
            b.readers = []
        for b in reads:
            if b not in writes:
                b.readers.append(oid)
        return oid

    def emit(self, block, sems_cm):
        nc = self.nc
        ops = self.ops
        n = len(ops)
        eng_ops = {e: [] for e in ENGS}
        for i, o in enumerate(ops):
            eng_ops[o.eng].append(i)
        pos = {}
        for e in ENGS:
            k = 0
            for i in eng_ops[e]:
                if not ops[i].is_dma:
                    pos[i] = k
                    k += 1
        vc = [None] * n
        known = {e: {x: -1 for x in ENGS} for e in ENGS}
        known_dma = {e: set() for e in ENGS}
        for i, o in enumerate(ops):
            kc = known[o.eng]
            kd = known_dma[o.eng]
            for d in sorted(o.deps):
                od = ops[d]
                if od.is_dma:
                    if d in kd:
                        continue
                    o.waits.append(d)
                    kd.add(d)
                else:
                    if od.eng == o.eng and o.eng == "pe":
                        continue
                    if kc[od.eng] >= pos[d]:
                        continue
                    o.waits.append(d)
                    od.needed = True
                dc = vc[d]
                for x in ENGS:
                    if dc[x] > kc[x]:
                        kc[x] = dc[x]
            c = dict(kc)
            if not o.is_dma:
                if pos[i] > c[o.eng]:
                    c[o.eng] = pos[i]
            vc[i] = c

        def new_sem(name):
            return sems_cm.enter_context(nc.semaphore(name))

        for e in ENGS:
            cur = None
            cnt = 0
            k = 0
            for i in eng_ops[e]:
                o = ops[i]
                if o.is_dma or not o.needed:
                    continue
                if cur is None or cnt >= SEM_LIMIT:
                    cur = new_sem(f"s_{e}_{k}_{id(self) % 997}")
                    k += 1
                    cnt = 0
                cnt += 1
                o.tok = (cur, cnt)
        caps = {"sp": 10, "pool": 12, "act": 4, "dve": 2, "pe": 2}
        for e in ENGS:
            dl = [i for i in eng_ops[e] if ops[i].is_dma]
            if not dl:
                continue
            nsem = min(len(dl), caps[e])
            dsems = [new_sem(f"s_dma_{e}_{j}_{id(self) % 997}") for j in range(nsem)]
            dcnt = [0] * nsem
            dprev = [None] * nsem
            for j, i in enumerate(dl):
                o = ops[i]
                s = j % nsem
                if dprev[s] is not None and dprev[s] not in o.waits:
                    o.waits.append(dprev[s])
                dcnt[s] += 16
                o.tok = (dsems[s], dcnt[s])
                dprev[s] = i
        self.stats = {e: len(eng_ops[e]) for e in ENGS}
        self.stats["waits"] = sum(len(o.waits) for o in ops)
        self.stats["incs"] = sum(1 for o in ops if o.tok is not None)

        def run_engine(e, eng):
            for i in eng_ops[e]:
                o = ops[i]
                seen = {}
                for d in o.waits:
                    sem, val = ops[d].tok
                    key = id(sem)
                    if key not in seen or seen[key][1] < val:
                        seen[key] = (sem, val)
                for sem, val in seen.values():
                    eng.wait_ge(sem, val)
                ins = o.fn(eng)
                if o.tok is not None and ins is not None:
                    ins.then_inc(o.tok[0], 16 if o.is_dma else 1)

        @block.tensor
        def _(eng):
            run_engine("pe", eng)

        @block.scalar
        def _(eng):
            run_engine("act", eng)

        @block.vector
        def _(eng):
            run_engine("dve", eng)

        @block.gpsimd
        def _(eng):
            run_engine("pool", eng)

        @block.sync
        def _(eng):
            run_engine("sp", eng)


class K:
    def __init__(self, P):
        self.P = P

    def mm(self, out, lhsT, rhs, start, stop, R, W):
        self.P.op("pe", lambda e: e.matmul(out, lhsT=lhsT, rhs=rhs, start=start, stop=stop), R, W)

    def tr(self, out, in_, ident, R, W):
        self.P.op("pe", lambda e: e.transpose(out, in_, ident), R, W)

    def act(self, out, in_, func, R, W, scale=1.0, bias=None, accum=None):
        def fn(e):
            kw = {}
            if bias is not None:
                kw["bias"] = bias
            if accum is not None:
                kw["accum_out"] = accum
            return e.activation(out=out, in_=in_, func=func, scale=scale, **kw)
        self.P.op("act", fn, R, W)

    def ts(self, eng, out, in0, s1, s2, op0, op1, R, W):
        def fn(e):
            if op1 is None:
                return e.tensor_scalar(out=out, in0=in0, scalar1=s1, scalar2=None, op0=op0)
            return e.tensor_scalar(out=out, in0=in0, scalar1=s1, scalar2=s2, op0=op0, op1=op1)
        self.P.op(eng, fn, R, W)

    def stt(self, out, in0, scalar, in1, op0, op1, R, W):
        self.P.op("dve", lambda e: e.scalar_tensor_tensor(out=out, in0=in0, scalar=scalar, in1=in1,
                                                          op0=op0, op1=op1), R, W)

    def tt(self, eng, out, in0, in1, op, R, W):
        self.P.op(eng, lambda e: e.tensor_tensor(out=out, in0=in0, in1=in1, op=op), R, W)

    def cp(self, eng, out, in_, R, W):
        if eng == "act":
            self.P.op("act", lambda e: e.activation(out=out, in_=in_, func=AF.Copy), R, W)
        else:
            self.P.op(eng, lambda e: e.tensor_copy(out=out, in_=in_), R, W)

    def recip(self, out, in_, R, W):
        self.P.op("dve", lambda e: e.reciprocal(out=out, in_=in_), R, W)

    def memset(self, eng, ap, val, W):
        self.P.op(eng, lambda e: e.memset(ap, val), (), W)

    def dma(self, eng, out, in_, R, W):
        self.P.op(eng, lambda e: e.dma_start(out=out, in_=in_), R, W, dma=True)

    def fence(self, eng, R):
        self.P.op(eng, lambda e: None, R, ())


def build(NSEQ, S):
    NT = NSEQ * S
    NB = S // 128
    TPS = S // 512
    nc = bass.Bass("TRN2", target_bir_lowering=False)

    def din(name, shape, dt=F32):
        return nc.dram_tensor(name, list(shape), dt, kind="ExternalInput").ap()

    x_d = din("x", [NT, D])
    cT_d = din("cT", [128, 8, NSEQ])
    wada_d = din("w_ada", [D, 6 * D])
    bada_d = din("b_ada", [1, 6 * D])
    win_d = din("w_in", [D, INC])
    wout_d = din("w_out", [D, D])
    wup_d = din("w_up", [D, 2 * DFF])
    wdn_d = din("w_down", [DFF, D])
    pmg_d = din("pmg", [128, 8])
    pfg_d = din("pfg", [128, 8])
    post4_d = din("post4", [NSEQ, 2, D])
    mlcw_d = din("mlcw", [128, 8, 4])
    mlcb_d = din("mlcb", [128, 8])
    ib_d = din("ib16", [128, 16])
    fb_d = din("fb16", [128, 16])
    mlg_d = din("mlg", [128, 512])
    sink_d = din("sinks", [128, 8])
    attg_d = din("attg", [128, 512])
    ffw_d = din("ffw", [128, 2 * NCH, 3])
    ffb_d = din("ffb", [128, 2 * NCH])
    identb_d = din("identb", [128, 128], BF16)
    tri_d = din("triu", [128, 128])
    maskU_d = din("masku", [128, 128])
    maskC_d = din("maskc", [128, 512], BF16)
    maskP_d = din("maskp", [128, 512], BF16)
    cos_d = din("cos", [128, NB, 32])
    sin_d = din("sin", [128, NB, 32])
    i4_d = din("i4", [4, 4])
    out_d = nc.dram_tensor("out", [NT, D], F32, kind="ExternalOutput").ap()
    x1s = nc.dram_tensor("x1s", [NT, D], F32).ap()
    gsc = nc.dram_tensor("gsc", [2, NSEQ, D], F32).ap()

    with ExitStack() as outer:
        def sb(name, shape, dt=F32):
            return outer.enter_context(nc.sbuf_tensor("s_" + name, list(shape), dt))

        G2 = sb("G2", [128, 8, NSEQ])
        SH2 = sb("SH2", [128, 8, NSEQ])
        identb = sb("identb", [128, 128], BF16)
        neghalf = sb("neghalf", [128, 16])
        banks = [outer.enter_context(nc.psum_tensor(f"bank{i}", [128, 512], F32)) for i in range(8)]

        with ExitStack() as ph:
            def sb1(name, shape, dt=F32):
                return ph.enter_context(nc.sbuf_tensor("t_" + name, list(shape), dt))

            P = Prog(nc)
            k = K(P)
            B = lambda n: Buf(n)

            Win = sb1("Win", [128, 8, INC], BF16)
            Wout = sb1("Wout", [128, 8, D], BF16)
            b_Win = [B(f"Win{i}") for i in range(8)]
            b_Wout = [B(f"Wout{i}") for i in range(8)]
            for i in range(8):
                k.dma("pool", Win[:, i, :], win_d[i * 128:(i + 1) * 128, :], (), [b_Win[i]])
            for i in range(8):
                k.dma("pool", Wout[:, i, :], wout_d[i * 128:(i + 1) * 128, :], (), [b_Wout[i]])

            RC = []
            b_cT = B("c_cT")
            cT = sb1("cT", [128, 8, NSEQ])
            k.dma("sp", cT[:], cT_d, (), [b_cT])
            b_i4 = B("c_i4")
            i4 = sb1("i4", [4, 4])
            k.dma("sp", i4[:], i4_d, (), [b_i4])
            onesf = sb1("onesf", [128, 128])
            onesb = sb1("onesb", [128, 2], BF16)
            b_c2 = B("c_misc")
            k.memset("dve", onesf[:], 1.0, [b_c2])
            k.memset("dve", onesb[:], 1.0, [b_c2])
            k.memset("dve", neghalf[:], -0.5, [b_c2])
            b_bk = [Buf(f"bk{i}", excl=True) for i in range(8)]
            RC0 = [b_cT, b_i4, b_c2]
            consts = {}

            def load_consts():
                b_id = B("c_ident")
                RC.append(b_id)
                k.dma("sp", identb[:], identb_d, (), [b_id])
                for (nm, src, shp, dt) in ():
                    pass
            const_srcs = [(nm, src) for (nm, src, shp, dt) in (("pmg", pmg_d, [128, 8], F32), ("pfg", pfg_d, [128, 8], F32),
                                           ("mlcw", mlcw_d, [128, 8, 4], F32), ("mlcb", mlcb_d, [128, 8], F32),
                                           ("ib16", ib_d, [128, 16], F32), ("fb16", fb_d, [128, 16], F32),
                                           ("mlgh", mlg_d, [128, 512], F32), ("esink", sink_d, [128, 8], F32),
                                           ("attg", attg_d, [128, 512], F32), ("triu", tri_d, [128, 128], F32),
                                           ("masku", maskU_d, [128, 128], F32), ("maskc", maskC_d, [128, 512], BF16),
                                           ("maskp", maskP_d, [128, 512], BF16), ("cosT", cos_d, [128, NB, 32], F32),
                                           ("sinT", sin_d, [128, NB, 32], F32))]

            def load_some_consts(cnt):
                for _ in range(cnt):
                    if const_srcs:
                        nm, src = const_srcs.pop(0)
                        k.dma("sp", consts[nm][:], src, (), [consts["b_" + nm]])

            for (nm, shp, dt) in (("pmg", [128, 8], F32), ("pfg", [128, 8], F32), ("mlcw", [128, 8, 4], F32),
                                  ("mlcb", [128, 8], F32), ("ib16", [128, 16], F32), ("fb16", [128, 16], F32),
                                  ("mlgh", [128, 512], F32), ("esink", [128, 8], F32), ("attg", [128, 512], F32),
                                  ("triu", [128, 128], F32), ("masku", [128, 128], F32), ("maskc", [128, 512], BF16),
                                  ("maskp", [128, 512], BF16), ("cosT", [128, NB, 32], F32), ("sinT", [128, NB, 32], F32)):
                consts[nm] = sb1(nm, shp, dt)
                consts["b_" + nm] = B("c_" + nm)
                RC.append(consts["b_" + nm])
            pmg, pfg, mlcw, mlcb, ib16, fb16 = (consts[n_] for n_ in ("pmg", "pfg", "mlcw", "mlcb", "ib16", "fb16"))
            mlgh, esink, attg, triu, masku = (consts[n_] for n_ in ("mlgh", "esink", "attg", "triu", "masku"))
            maskc, maskp, cosT, sinT = (consts[n_] for n_ in ("maskc", "maskp", "cosT", "sinT"))
            RC += RC0
            sT = sb1("sT", [128, 8, NSEQ])
            b_sT = B("sT")
            k.act(sT[:], cT[:], AF.Tanh, RC0, [b_sT], scale=0.5)
            k.ts("dve", sT[:], sT[:], 0.5, 0.5, ALU.mult, ALU.add, [b_sT], [b_sT])
            k.tt("dve", sT[:], sT[:], cT[:], ALU.mult, [b_sT] + RC0, [b_sT])
            modc = sb1("modc", [NSEQ, 1, 512])
            b_mc = B("modc")
            b_modc = [b_mc, b_mc]
            badar = sb1("badar", [1, 512])
            postc = sb1("postc", [NSEQ, 512])
            wring = sb1("wring", [128, 2, 2, 512])
            b_wr = [B("wr0"), B("wr1")]
            b_bd = B("bd")
            b_pc = B("pc")
            pmt = banks[3][:, 0:32 * NSEQ].rearrange("p (c b) -> p c b", b=NSEQ)
            b_gsc = [[B(f"gsc{i}_{q}") for q in range(2)] for i in range(2)]
            grp = {0: 0, 1: 1, 3: 2, 4: 3}
            for n in range(12):
                m_ = n % 2
                c0 = n * 512
                g1024 = n // 2
                pm = banks[1 + m_][0:NSEQ, :]
                k.dma("sp", badar[:], bada_d[0:1, c0:c0 + 512], (), [b_bd])
                for kp in range(4):
                    s_ = (n * 4 + kp) % 2
                    k.dma("sp", wring[:, s_, :, :],
                          wada_d[kp * 256:(kp + 1) * 256, c0:c0 + 512].rearrange("(k p) n -> p k n", p=128), (), [b_wr[s_]])
                    for kk in range(2):
                        k.mm(pm, sT[:, kp * 2 + kk, :], wring[:, s_, kk, :], kp == 0 and kk == 0, False,
                             [b_sT, b_wr[s_]], [b_bk[1 + m_]])
                if n == 0:
                    load_consts()
                load_some_consts(2)
                k.mm(pm, onesf[0:1, 0:NSEQ], badar[0:1, :], False, True, RC0 + [b_bd], [b_bk[1 + m_]])
                k.cp("act", modc[:, 0, :], pm, [b_bk[1 + m_]], [b_modc[m_]])
                if g1024 in grp:
                    for c in range(4):
                        k.mm(pmt[:, grp[g1024] * 8 + m_ * 4 + c, :], modc[:, 0, c * 128:(c + 1) * 128],
                             i4[0:NSEQ, 0:NSEQ], True, True, [b_modc[m_]] + RC0, [b_bk[3]])
                else:
                    gi = 0 if g1024 == 2 else 1
                    k.dma("sp", postc[:], post4_d[:, gi, m_ * 512:(m_ + 1) * 512], (), [b_pc])
                    k.tt("dve", modc[:, 0, :], modc[:, 0, :], postc[:], ALU.mult, [b_modc[m_], b_pc], [b_modc[m_]])
                    k.dma("sp", gsc[gi][:, m_ * 512:(m_ + 1) * 512], modc[:, 0, :], [b_modc[m_]], [b_gsc[gi][m_]])
            b_c3 = B("c_esink2")
            k.ts("dve", mlgh[:], mlgh[:], 0.5, None, ALU.mult, None, [consts["b_mlgh"]], [consts["b_mlgh"]])
            k.act(esink[:], esink[:], AF.Exp, [consts["b_esink"]], [consts["b_esink"]])
            G1 = sb1("G1", [128, 8, NSEQ])
            SH1 = sb1("SH1", [128, 8, NSEQ])
            b_G = B("G")
            k.cp("dve", SH1[:], pmt[:, 0:8, :], [b_bk[3]], [b_G])
            k.stt(G1[:], pmt[:, 8:16, :], 1.0, pmg[:].unsqueeze(2).broadcast_to([128, 8, NSEQ]),
                  ALU.add, ALU.mult, [b_bk[3]] + RC, [b_G])
            k.cp("dve", SH2[:], pmt[:, 16:24, :], [b_bk[3]], [b_G])
            k.stt(G2[:], pmt[:, 24:32, :], 1.0, pfg[:].unsqueeze(2).broadcast_to([128, 8, NSEQ]),
                  ALU.add, ALU.mult, [b_bk[3]] + RC, [b_G])

            XB = sb1("XB", [128, 8, D])
            b_XB = [B(f"XB{i}") for i in range(8)]
            GP1 = sb1("GP1", [128, 1, D])
            b_GP1 = B("GP1")
            hT = sb1("hT", [128, 8, 512], BF16)
            b_hT = [B(f"hT{j}") for j in range(4)]
            qkT = sb1("qkT", [128, 8, 512], BF16)
            b_qkT = [B(f"qkT{c}") for c in range(8)]
            U = sb1("U", [128, 2, 515])
            b_U = [B("U0"), B("U1")]
            acc = sb1("acc", [128, 2, 512])
            b_acc = [B("acc0"), B("acc1")]
            halo = sb1("halo", [128, 8, 3])
            b_halo = B("halo")
            xn = sb1("xn", [128, 2, D], BF16)
            b_xn = [B("xn0"), B("xn1")]
            junk = sb1("junk", [128, D], BF16)
            b_junk = B("junk")
            st = sb1("st", [128, 16])
            b_st = B("st")
            gat = sb1("gat", [128, 8, 16])
            b_gat = B("gat")
            sd = sb1("sd", [128, 16])
            eab = sb1("eab", [128, 16], BF16)
            tanho = sb1("tanho", [128, 512], BF16)
            b_tanho = B("tanho")
            gso = sb1("gso", [128, 4, 512], BF16)
            b_gso = [B(f"gso{j}") for j in range(4)]
            vtil = sb1("vtil", [128, 4, 4, 128], BF16)
            b_vtil = [B(f"vtil{j}") for j in range(4)]
            STm = sb1("STm", [128, 4, 4, 128], BF16)
            b_STm = [B(f"STm{j}") for j in range(4)]
            ktok = sb1("ktok", [128, 4, 4, 128], BF16)
            b_ktok = [B(f"ktok{j}") for j in range(4)]
            CT = sb1("CT", [128, 4, 128])
            CTb = sb1("CTb", [128, 4, 128], BF16)
            nv = sb1("nv", [128, 4])
            nvt = sb1("nvt", [128, 4])
            nvb = sb1("nvb", [128, 4], BF16)
            b_CT = B("CT")
            b_CTb = B("CTb")
            b_nv = B("nv")
            b_nvb = B("nvb")
            mst = sb1("mst", [128, 2, 8, 4])
            b_mst = [B("mst0"), B("mst1")]
            rt1 = sb1("rt1", [128, 640])
            rt2 = sb1("rt2", [128, 640])
            rot = sb1("rot", [128, 4, 640], BF16)
            b_rt1, b_rt2 = B("rt1"), B("rt2")
            b_rot = [B(f"rot{j}") for j in range(4)]
            qTa = sb1("qTa", [128, 4, 4, 128], BF16)
            b_qTa = [B(f"qTa{j}") for j in range(4)]
            kTr = sb1("kTr", [128, 8, 128], BF16)
            b_kTr = [B(f"kTr{i}") for i in range(8)]
            vat = sb1("vat", [128, 8, 128], BF16)
            b_vat = [B(f"vat{i}") for i in range(8)]
            pT = sb1("pT", [128, 4, 512], BF16)
            b_pT = [B(f"pT{i}") for i in range(4)]
            ast = sb1("ast", [128, 2, 32])
            b_ast = [B("ast0"), B("ast1")]
            osb = sb1("osb", [128, 1, 512])
            b_os = B("osb")
            b_osb = [b_os, b_os]
            mix = sb1("mix", [128, 2, D], BF16)
            b_mixm, b_mixa = [B("mixm0"), B("mixm1")], [B("mixa0"), B("mixa1")]
            mixT = sb1("mixT", [128, 2, 8, 128], BF16)
            b_mixT = [B("mixT0"), B("mixT1")]
            yst = sb1("yst", [128, 2, 8])
            b_yst = [B("yst0"), B("yst1")]

            TR = banks[0][:].bitcast(BF16).rearrange("p (k n) -> p k n", k=8)
            b_bk0a = b_bk0b = b_bk[0]
            b_bk1, b_bk2, b_bk3, b_bk4, b_bk5, b_bk7 = b_bk[1], b_bk[2], b_bk[3], b_bk[4], b_bk[5], b_bk[7]
            QK = [banks[1][:], banks[2][:]]
            A0, A1 = banks[3][:], banks[4][:]
            Bt0, Bt1 = banks[5][:], banks[6][:, 0:256]
            b_bt1 = b_bk[6]
            GT = banks[6][:, 256:288].rearrange("p (j g) -> p j g", g=8)
            CSa = banks[6][:, 288:304]
            CSb = banks[6][:, 304:320]
            DN = banks[6][:, 320:324]
            DNs = banks[6][:, 324:328]
            SD = banks[6][:, 328:336]
            RTk = banks[6][:, 336:400].bitcast(BF16)
            b_GT = b_CS = b_DN = b_DNs = b_SD = b_RTk = b_bk[6]
            SC = banks[7][:]
            ST = banks[1][:].rearrange("p (h n) -> p h n", h=4)
            OE = banks[2][:].rearrange("p (h n) -> p h n", h=4)
            DC = banks[3][:].rearrange("p (h n) -> p h n", h=4)
            SO = banks[1][:].rearrange("p (h n) -> p h n", h=8)
            KT = banks[0][:, 0:256].bitcast(BF16).rearrange("p (h n) -> p h n", h=4)
            RTq = banks[0][:, 256:512].bitcast(BF16).rearrange("p (h n) -> p h n", h=4)

            x1_bufs = []

            def load_x_tile(b, ti):
                for j in range(4):
                    slot = ((b * TPS + ti) * 4 + j) % 8
                    t0 = b * S + ti * 512 + j * 128
                    k.dma("sp", XB[:, slot, :], x_d[t0:t0 + 128, :], (), [b_XB[slot]])

            tiles = [(b, ti) for b in range(NSEQ) for ti in range(TPS)]
            load_x_tile(*tiles[0])

            for tix, (b, ti) in enumerate(tiles):
                if tix + 1 < len(tiles):
                    load_x_tile(*tiles[tix + 1])
                if ti == 0:
                    k.dma("sp", GP1[:], gsc[0, b:b + 1, :].partition_broadcast(128), b_gsc[0], [b_GP1])
                    k.memset("pool", CT[:], 0.0, [b_CT])
                    k.memset("pool", CTb[:], 0.0, [b_CTb])
                    k.memset("pool", nv[:], 0.0, [b_nv])
                    k.memset("pool", nvb[:], 0.0, [b_nvb])
                    k.memset("pool", halo[:], 0.0, [b_halo])
                for j in range(4):
                    slot = (tix * 4 + j) % 8
                    first_dep = (b_gsc[0] + b_gsc[1]) if (tix == 0 and j == 0) else []
                    k.act(junk[:], XB[:, slot, :], AF.Square, [b_XB[slot]] + first_dep, [b_junk, b_st], accum=st[:, j:j + 1])
                k.ts("dve", st[:, 4:8], st[:, 0:4], 1.0 / D, EPS, ALU.mult, ALU.add, [b_st], [b_st])
                k.tt("pool", st[:, 8:12], st[:, 4:8], neghalf[:, 0:4], ALU.pow, [b_st] + RC, [b_st])
                TR7 = banks[7][:].bitcast(BF16).rearrange("p (k n) -> p k n", k=8)
                def emit_xn(j):
                    slot = (tix * 4 + j) % 8
                    k.act(xn[:, j % 2, :], XB[:, slot, :], AF.Identity, [b_XB[slot], b_st], [b_xn[j % 2]],
                          scale=st[:, 8 + j:9 + j])

                emit_xn(0)
                for j in range(4):
                    a_ = j % 2
                    TRj, bTR = (TR, [b_bk[0]]) if a_ == 0 else (TR7, [b_bk[7]])
                    for kk in range(8):
                        k.tr(TRj[:, kk, :], xn[:, a_, kk * 128:(kk + 1) * 128], identb[:], [b_xn[a_]] + RC, bTR)
                    if j + 1 < 4:
                        emit_xn(j + 1)
                    for kk in range(8):
                        k.ts("dve", hT[:, kk, j * 128:(j + 1) * 128], TRj[:, kk, :], G1[:, kk, b:b + 1],
                             SH1[:, kk, b:b + 1], ALU.mult, ALU.add, bTR + [b_G], [b_hT[j]])
                for j in range(4):
                    for kk in range(8):
                        k.mm(GT[:, j, :], hT[:, kk, j * 128:(j + 1) * 128], Win[:, kk, 2048:2056], kk == 0, kk == 7,
                             [b_hT[j], b_Win[kk]], [b_GT])
                g3 = lambda i: gat[:, i, :].rearrange("p (j h) -> p j h", h=4)
                k.tt("dve", g3(0), GT[:, :, 4:8], fb16[:].rearrange("p (j h) -> p j h", h=4), ALU.add, [b_GT] + RC, [b_gat])
                k.tt("dve", g3(1), GT[:, :, 0:4], ib16[:].rearrange("p (j h) -> p j h", h=4), ALU.add, [b_GT] + RC, [b_gat])
                k.act(gat[:, 2, :], gat[:, 0, :], AF.Exp, [b_gat], [b_gat], scale=-1.0)
                k.act(gat[:, 3, :], gat[:, 2, :], AF.Ln, [b_gat], [b_gat], bias=1.0)
                for c in range(8):
                    s = c % 2
                    bq = (b_bk1, b_bk2)[s]
                    for kk in range(8):
                        k.mm(QK[s], Win[:, kk, c * 128:(c + 1) * 128], hT[:, kk, :], kk == 0, kk == 7,
                             b_hT + [b_Win[kk]], [bq])
                    k.cp("act", U[:, s, 3:515], QK[s], [bq], [b_U[s]])
                    k.cp("pool", U[:, s, 0:3], halo[:, c, :], [b_halo], [b_U[s]])
                    k.cp("pool", halo[:, c, :], U[:, s, 512:515], [b_U[s]], [b_halo])
                    k.act(acc[:, s, :], QK[s], AF.Identity, [bq] + RC, [b_acc[s]], scale=mlcw[:, c, 3:4], bias=mlcb[:, c:c + 1])
                    for tap in (2, 1, 0):
                        k.stt(acc[:, s, :], U[:, s, tap:tap + 512], mlcw[:, c, tap:tap + 1], acc[:, s, :],
                              ALU.mult, ALU.add, [b_U[s], b_acc[s]] + RC, [b_acc[s]])
                    k.act(qkT[:, c, :], acc[:, s, :], AF.Silu, [b_acc[s]], [b_qkT[c]])
                k.mm(CSa, triu[:], gat[:, 3, :], True, True, [b_gat] + RC, [b_CS])
                k.mm(CSb, onesf[:], gat[:, 3, :], True, True, [b_gat] + RC, [b_CS])
                k.tt("dve", gat[:, 4, :], gat[:, 1, :], CSa, ALU.add, [b_gat, b_CS], [b_gat])
                k.act(gat[:, 5, :], gat[:, 4, :], AF.Exp, [b_gat], [b_gat])
                k.act(gat[:, 6, :], CSa, AF.Exp, [b_CS], [b_gat])
                k.act(gat[:, 7, :], CSb, AF.Exp, [b_CS], [b_gat], scale=-1.0)
                k.ts("dve", sd[:], gat[:, 7, :], KSCALE, None, ALU.mult, None, [b_gat], [b_gat])
                k.cp("dve", eab[:], gat[:, 5, :], [b_gat], [b_gat])
                ea, eb, dec = gat[:, 5, :], gat[:, 6, :], gat[:, 7, :]

                for j in range(4):
                    blk = ti * 4 + j
                    cols = slice(j * 128, (j + 1) * 128)
                    vs = blk % 8
                    for h in range(2):
                        outp, bb = (A0, b_bk3) if h == 0 else (A1, b_bk4)
                        for kk in range(8):
                            k.mm(outp, hT[:, kk, cols], Win[:, kk, 1024 + h * 512: 1536 + h * 512], kk == 0, kk == 7,
                                 [b_hT[j], b_Win[kk]], [bb])
                    for kk in range(8):
                        k.mm(Bt0, hT[:, kk, cols], Win[:, kk, 2056:2568], kk == 0, kk == 7, [b_hT[j], b_Win[kk]], [b_bk5])
                    for kk in range(8):
                        k.mm(Bt1, hT[:, kk, cols], Win[:, kk, 2568:2824], kk == 0, kk == 7, [b_hT[j], b_Win[kk]], [b_bt1])
                    for h in range(4):
                        k.act(vtil[:, j, h, :], A0[:, h * 128:(h + 1) * 128], AF.Identity, [b_bk3, b_gat], [b_vtil[j]],
                              scale=ea[:, j * 4 + h: j * 4 + h + 1])
                    k.act(tanho[:], A1, AF.Tanh, [b_bk4], [b_tanho], scale=0.5)
                    k.stt(gso[:, j, :], tanho[:], 1.0, mlgh[:], ALU.add, ALU.mult, [b_tanho] + RC, [b_gso[j]])
                    X4 = Bt0[:, 0:512].rearrange("p (h t d) -> p h t d", t=2, d=32)
                    Xk = Bt1[:, 0:128].rearrange("p (h t d) -> p h t d", t=2, d=32)
                    rotj = rot[:, j, :]
                    for (X, nh, off) in ((X4, 8, 0), (Xk, 2, 512)):
                        bsrc = b_bk5 if nh == 8 else b_bt1
                        t1v = rt1[:, off:off + nh * 64].rearrange("p (h t d) -> p h t d", t=2, d=32)
                        t2v = rt2[:, off:off + nh * 64].rearrange("p (h t d) -> p h t d", t=2, d=32)
                        rov = rotj[:, off:off + nh * 64].rearrange("p (h t d) -> p h t d", t=2, d=32)
                        cb = cosT[:, blk, :].unsqueeze(1).unsqueeze(1).broadcast_to([128, nh, 2, 32])
                        sb_ = sinT[:, blk, :].unsqueeze(1).broadcast_to([128, nh, 32])
                        k.tt("dve", t1v, X, cb, ALU.mult, [bsrc] + RC, [b_rt1])
                        k.tt("dve", t2v[:, :, 0, :], X[:, :, 1, :], sb_, ALU.mult, [bsrc] + RC, [b_rt2])
                        k.tt("dve", t2v[:, :, 1, :], X[:, :, 0, :], sb_, ALU.mult, [bsrc] + RC, [b_rt2])
                        if nh == 8:
                            pv = lambda t, i: t[:, off:off + 512].rearrange("p (g r t d) -> p g r t d", g=2, r=4, t=2)[:, :, :, i, :]
                            po = lambda i: rotj[:, 0:512].rearrange("p (r g t d) -> p g r t d", g=2, r=4, t=2)[:, :, :, i, :]
                            k.tt("pool", po(0), pv(rt1, 0), pv(rt2, 0), ALU.subtract, [b_rt1, b_rt2], [b_rot[j]])
                            k.tt("pool", po(1), pv(rt1, 1), pv(rt2, 1), ALU.add, [b_rt1, b_rt2], [b_rot[j]])
                        else:
                            k.tt("pool", rov[:, :, 0, :], t1v[:, :, 0, :], t2v[:, :, 0, :], ALU.subtract, [b_rt1, b_rt2], [b_rot[j]])
                            k.tt("pool", rov[:, :, 1, :], t1v[:, :, 1, :], t2v[:, :, 1, :], ALU.add, [b_rt1, b_rt2], [b_rot[j]])
                    k.cp("act", vat[:, vs, :], Bt1[:, 128:256], [b_bt1], [b_vat[vs]])
                for j in range(4):
                    blk = ti * 4 + j
                    cols = slice(j * 128, (j + 1) * 128)
                    vs = blk % 8
                    if j % 2 == 0:
                        STv, bST = banks[1][:].rearrange("p (h n) -> p h n", h=4), [b_bk1]
                        KTv, bKT = KT, [b_bk0a]
                        RTv, bRT = RTq, [b_bk0b]
                    else:
                        STv, bST = banks[2][:].rearrange("p (h n) -> p h n", h=4), [b_bk2]
                        KTv = banks[7][:, 0:256].bitcast(BF16).rearrange("p (h n) -> p h n", h=4)
                        RTv = banks[7][:, 256:512].bitcast(BF16).rearrange("p (h n) -> p h n", h=4)
                        bKT = bRT = [b_bk7]
                    for h in range(4):
                        k.mm(STv[:, h, :], qkT[:, 4 + h, cols], qkT[:, h, cols], True, True, [b_qkT[4 + h], b_qkT[h]], bST)
                    k.tt("dve", STm[:, j, :, :], STv, masku[:].unsqueeze(1).broadcast_to([128, 4, 128]), ALU.mult,
                         bST + RC, [b_STm[j]])
                    for h in range(4):
                        k.tr(KTv[:, h, :], qkT[:, 4 + h, cols], identb[:], [b_qkT[4 + h]] + RC, bKT)
                    k.cp("act", ktok[:, j, :, :], KTv, bKT, [b_ktok[j]])
                    for pr in range(4):
                        k.tr(RTv[:, pr, :], rot[:, j, pr * 128:(pr + 1) * 128], identb[:], [b_rot[j]] + RC, bRT)
                    k.tr(RTk, rot[:, j, 512:640], identb[:], [b_rot[j]] + RC, [b_RTk])
                    k.cp("act", qTa[:, j, :, :], RTv, bRT, [b_qTa[j]])
                    k.cp("act", kTr[:, vs, :], RTk, [b_RTk], [b_kTr[vs]])

                def o_transposes(j):
                    a = j % 2
                    for kk in range(8):
                        k.tr(TR[:, kk, :], mix[:, a, kk * 128:(kk + 1) * 128], identb[:], [b_mixm[a], b_mixa[a]] + RC,
                             [b_bk0a, b_bk0b])
                    k.cp("act", mixT[:, a, :, :], TR, [b_bk0a, b_bk0b], [b_mixT[a]])

                def o_rest(j):
                    a = j % 2
                    blk = ti * 4 + j
                    slot = (tix * 4 + j) % 8
                    for h in range(2):
                        outp, bb = (A1, b_bk4) if h == 0 else (Bt0, b_bk5)
                        for kk in range(8):
                            k.mm(outp, mixT[:, a, kk, :], Wout[:, kk, h * 512:(h + 1) * 512], kk == 0, kk == 7,
                                 [b_mixT[a], b_Wout[kk]], [bb])
                    ys = yst[:, a, :]
                    k.act(junk[:, 0:512], A1, AF.Square, [b_bk4], [b_junk, b_yst[a]], accum=ys[:, 0:1])
                    k.act(junk[:, 512:1024], Bt0, AF.Square, [b_bk5], [b_junk, b_yst[a]], accum=ys[:, 1:2])
                    k.tt("dve", ys[:, 2:3], ys[:, 0:1], ys[:, 1:2], ALU.add, [b_yst[a]], [b_yst[a]])
                    k.ts("dve", ys[:, 3:4], ys[:, 2:3], 1.0 / D, EPS, ALU.mult, ALU.add, [b_yst[a]], [b_yst[a]])
                    k.tt("pool", ys[:, 4:5], ys[:, 3:4], neghalf[:, 0:1], ALU.pow, [b_yst[a]] + RC, [b_yst[a]])
                    xb = XB[:, slot, :]
                    for h, (outp, bb) in enumerate(((A1, b_bk4), (Bt0, b_bk5))):
                        k.stt(outp, outp, ys[:, 4:5], GP1[:, 0, h * 512:(h + 1) * 512], ALU.mult, ALU.mult,
                              [bb, b_yst[a], b_GP1], [bb])
                        k.tt("dve", xb[:, h * 512:(h + 1) * 512], outp, xb[:, h * 512:(h + 1) * 512], ALU.add,
                             [bb, b_XB[slot]], [b_XB[slot]])
                    t0 = b * S + blk * 128
                    bx1 = B(f"x1_{t0}")
                    x1_bufs.append(bx1)
                    k.dma("sp", x1s[t0:t0 + 128, :], xb, [b_XB[slot]], [bx1])

                for j in range(4):
                    blk = ti * 4 + j
                    cols = slice(j * 128, (j + 1) * 128)
                    vs, pvs = blk % 8, (blk - 1) % 8
                    a = j % 2
                    kbs = ([(pvs, maskp)] if blk > 0 else []) + [(vs, maskc)]
                    scl = [(g, kb, msk) for g in range(2) for (kb, msk) in kbs]
                    pidx = {(g, kb): i for i, (g, kb, _) in enumerate(scl)}

                    def emit_sc(i):
                        if i >= len(scl):
                            return
                        g, kb, msk = scl[i]
                        k.mm(SC, kTr[g * 64:(g + 1) * 64, kb, :],
                             qTa[g * 64:(g + 1) * 64, j, :, :].rearrange("p h n -> p (h n)"), True, False,
                             [b_kTr[kb], b_qTa[j]], [b_bk7])
                        k.mm(SC, identb[:], msk[:], False, True, RC, [b_bk7])
                        k.act(pT[:, i, :], SC, AF.Exp, [b_bk7], [b_pT[i]], scale=0.125)

                    emit_sc(0)
                    for h in range(4):
                        k.mm(DC[:, h, :], ktok[:, j, h, :], vtil[:, j, h, :], True, True, [b_ktok[j], b_vtil[j]], [b_bk3])
                        k.mm(DNs[:, h:h + 1], ktok[:, j, h, :], eab[:, j * 4 + h: j * 4 + h + 1], True, True,
                             [b_ktok[j], b_gat], [b_DNs])
                    emit_sc(1)
                    for h in range(4):
                        k.mm(OE[:, h, :], qkT[:, h, cols], CTb[:, h, :], True, False, [b_qkT[h], b_CTb], [b_bk2])
                        k.mm(OE[:, h, :], STm[:, j, h, :], vtil[:, j, h, :], False, True, [b_STm[j], b_vtil[j]], [b_bk2])
                    for h in range(4):
                        k.mm(DN[:, h:h + 1], qkT[:, h, cols], nvb[:, h:h + 1], True, False, [b_qkT[h], b_nvb], [b_DN])
                        k.mm(DN[:, h:h + 1], STm[:, j, h, :], eab[:, j * 4 + h: j * 4 + h + 1], False, True,
                             [b_STm[j], b_gat], [b_DN])
                    emit_sc(2)
                    emit_sc(3)
                    for h in range(4):
                        k.ts("dve", CT[:, h, :], CT[:, h, :], dec[:, j * 4 + h: j * 4 + h + 1], None, ALU.mult, None,
                             [b_CT, b_gat], [b_CT])
                        k.stt(CT[:, h, :], DC[:, h, :], sd[:, j * 4 + h: j * 4 + h + 1], CT[:, h, :], ALU.mult, ALU.add,
                              [b_bk3, b_gat, b_CT], [b_CT])
                    ms_ = mst[:, a, :, :]
                    bm = b_mst[a]
                    k.act(ms_[:, 0, :], DN, AF.Abs, [b_DN], [bm])
                    k.tt("dve", ms_[:, 1, :], ms_[:, 0, :], eb[:, j * 4:(j + 1) * 4], ALU.max, [bm, b_gat], [bm])
                    k.tt("dve", ms_[:, 2, :], ms_[:, 1, :], ms_[:, 1, :], ALU.mult, [bm], [bm])
                    k.tt("dve", nv[:], nv[:], dec[:, j * 4:(j + 1) * 4], ALU.mult, [b_nv, b_gat], [b_nv])
                    k.tt("dve", nvt[:], DNs, sd[:, j * 4:(j + 1) * 4], ALU.mult, [b_DNs, b_gat], [b_nv])
                    k.tt("dve", nv[:], nv[:], nvt[:], ALU.add, [b_nv], [b_nv])
                    k.cp("dve", nvb[:], nv[:], [b_nv], [b_nvb])
                    for h in range(4):
                        k.act(junk[:, h * 128:(h + 1) * 128], OE[:, h, :], AF.Square, [b_bk2], [b_junk, bm],
                              accum=ms_[:, 3, h:h + 1])
                    k.ts("dve", ms_[:, 4, :], ms_[:, 3, :], 1.0 / 128.0, None, ALU.mult, None, [bm], [bm])
                    k.stt(ms_[:, 5, :], ms_[:, 2, :], EPS, ms_[:, 4, :], ALU.mult, ALU.add, [bm], [bm])
                    k.tt("pool", ms_[:, 6, :], ms_[:, 5, :], neghalf[:, 0:4], ALU.pow, [bm] + RC, [bm])
                    k.cp("act", CTb[:], CT[:], [b_CT], [b_CTb])
                    for h in range(4):
                        k.stt(mix[:, a, h * 128:(h + 1) * 128], OE[:, h, :], ms_[:, 6, h:h + 1],
                              gso[:, j, h * 128:(h + 1) * 128], ALU.mult, ALU.mult, [b_bk2, bm, b_gso[j]], [b_mixm[a]])
                    if j > 0:
                        o_transposes(j - 1)
                        o_rest(j - 1)
                    for g in range(2):
                        for r in range(4):
                            hh = g * 4 + r
                            for ii, (kb, _) in enumerate(kbs):
                                pi = pidx[(g, kb)]
                                k.mm(SO[:, hh, :], pT[:, pi, r * 128:(r + 1) * 128], vat[:, kb, g * 64:(g + 1) * 64],
                                     ii == 0, ii == len(kbs) - 1, [b_pT[pi], b_vat[kb]], [b_bk1])
                            for ii, (kb, _) in enumerate(kbs):
                                pi = pidx[(g, kb)]
                                k.mm(SD[:, hh:hh + 1], pT[:, pi, r * 128:(r + 1) * 128], onesb[:, 0:1],
                                     ii == 0, ii == len(kbs) - 1, [b_pT[pi]] + RC, [b_SD])
                    as_ = ast[:, a, :]
                    ba = b_ast[a]
                    ob = osb[:, 0, :]
                    k.tt("dve", as_[:, 0:8], SD, esink[:], ALU.add, [b_SD] + RC, [ba])
                    k.recip(as_[:, 8:16], as_[:, 0:8], [ba], [ba])
                    k.tt("dve", ob.rearrange("p (h d) -> p h d", d=64), SO,
                         as_[:, 8:16].unsqueeze(2).broadcast_to([128, 8, 64]), ALU.mult, [b_bk1, ba], [b_osb[a]])
                    k.act(junk[:, 512:1024], ob, AF.Square, [b_osb[a]], [b_junk, ba], accum=as_[:, 16:17])
                    k.ts("dve", as_[:, 17:18], as_[:, 16:17], 1.0 / 512.0, EPS, ALU.mult, ALU.add, [ba], [ba])
                    k.tt("pool", as_[:, 18:19], as_[:, 17:18], neghalf[:, 0:1], ALU.pow, [ba] + RC, [ba])
                    k.stt(mix[:, a, 512:1024], ob, as_[:, 18:19], attg[:], ALU.mult, ALU.mult, [b_osb[a], ba] + RC, [b_mixa[a]])
                o_transposes(3)
                o_rest(3)

            k.fence("sp", x1_bufs + b_gsc[0] + b_gsc[1] + [b_G])
            with ExitStack() as sems, nc.Block() as block:
                P.emit(block, sems)
            stats1 = P.stats

        with ExitStack() as ph:
            def sb2(name, shape, dt=F32):
                return ph.enter_context(nc.sbuf_tensor("t_" + name, list(shape), dt))

            P = Prog(nc)
            k = K(P)
            B = lambda n: Buf(n)
            Wup = sb2("Wup", [128, 8, 2 * DFF], BF16)
            Wdn = sb2("Wdn", [128, NCH, D], BF16)
            NCG = (NCH + 3) // 4
            b_Wup = [[B(f"Wup{h}_{g}") for g in range(NCG)] for h in range(2)]
            b_Wdn = [B(f"Wdn{i}") for i in range(NCH)]
            def load_wup(cg):
                c_lo, c_hi = cg * 4, min(NCH, cg * 4 + 4)
                w_ = (c_hi - c_lo) * 128
                for h in range(2):
                    col0 = h * DFF + c_lo * 128
                    k.dma("pool", Wup[:, :, col0:col0 + w_],
                          wup_d[:, col0:col0 + w_].rearrange("(k p) n -> p k n", p=128), (), [b_Wup[h][cg]])

            def load_rest_weights():
                for cg in range(1, NCG):
                    load_wup(cg)

            def load_wdn(i):
                k.dma("pool", Wdn[:, i, :], wdn_d[i * 128:(i + 1) * 128, :], (), [b_Wdn[i]])

            load_wup(0)
            RC = [B("c_ffw"), B("c_ffb")]
            ffw = sb2("ffw", [128, 2 * NCH, 3])
            ffb = sb2("ffb", [128, 2 * NCH])
            k.dma("sp", ffw[:], ffw_d, (), [RC[0]])
            k.dma("sp", ffb[:], ffb_d, (), [RC[1]])
            XR = sb2("XR", [128, 2, D])
            b_XR = [B("XR0"), B("XR1")]
            XO = sb2("XO", [128, 2, D])
            b_XO = [B("XO0"), B("XO1")]
            GP2 = sb2("GP2", [128, 1, D])
            b_GP2 = B("GP2")
            hT2 = sb2("hT2", [128, 8, 512], BF16)
            b_hT2 = [B(f"hT2{j}") for j in range(4)]
            aT = sb2("aT", [128, NCH, 512], BF16)
            b_aT = [B(f"aT{c}") for c in range(NCH)]
            xn2 = sb2("xn2", [128, D], BF16)
            b_xn2 = B("xn2")
            junk2 = sb2("junk2", [128, D], BF16)
            b_junk2 = B("junk2")
            Ug = sb2("Ug", [128, 514])
            Uv = sb2("Uv", [128, 514])
            accg = sb2("accg", [128, 512])
            accv = sb2("accv", [128, 512])
            gl = sb2("gl", [128, 512])
            b_Ug, b_Uv, b_accg, b_accv, b_gl = B("Ug"), B("Uv"), B("accg"), B("accv"), B("gl")
            halo2 = sb2("halo2", [128, 2 * NCH, 2])
            b_halo2 = B("halo2")
            st2 = sb2("st2", [128, 16])
            b_st2 = B("st2")
            yst2 = sb2("yst2", [128, 8])
            b_yst2 = B("yst2")

            TR2 = banks[0][:].bitcast(BF16).rearrange("p (k n) -> p k n", k=8)
            b_tr2 = Buf("tr2", excl=True)
            PG = [banks[1][:], banks[3][:]]
            PV = [banks[2][:], banks[4][:]]
            b_PG = [Buf("PG0", excl=True), Buf("PG1", excl=True)]
            b_PV = [Buf("PV0", excl=True), Buf("PV1", excl=True)]
            Y2 = [banks[5][:], banks[6][:]]
            b_Y2 = [Buf("Y20", excl=True), Buf("Y21", excl=True)]

            out_bufs = []
            tiles = [(b, ti) for b in range(NSEQ) for ti in range(TPS)]

            xr_loaded = set()
            xo_loaded = set()

            def load_xr(tix, j):
                b_, ti_ = tiles[tix]
                t0_ = b_ * S + ti_ * 512 + j * 128
                s_ = (tix * 4 + j) % 2
                k.dma("sp", XR[:, s_, :], x1s[t0_:t0_ + 128, :], (), [b_XR[s_]])
                xr_loaded.add((tix, j))

            def load_xo(tix, j):
                b_, ti_ = tiles[tix]
                t0_ = b_ * S + ti_ * 512 + j * 128
                s_ = (tix * 4 + j) % 2
                k.dma("sp", XO[:, s_, :], x1s[t0_:t0_ + 128, :], (), [b_XO[s_]])
                xo_loaded.add((tix, j))

            def norm_a(tix, j):
                b, ti = tiles[tix]
                t0 = b * S + ti * 512 + j * 128
                s = (tix * 4 + j) % 2
                if (tix, j) not in xr_loaded:
                    load_xr(tix, j)
                xr = XR[:, s, :]
                k.act(junk2[:], xr, AF.Square, [b_XR[s]], [b_junk2, b_st2], accum=st2[:, 0:1])
                k.ts("dve", st2[:, 1:2], st2[:, 0:1], 1.0 / D, EPS, ALU.mult, ALU.add, [b_st2], [b_st2])
                k.tt("pool", st2[:, 2:3], st2[:, 1:2], neghalf[:, 0:1], ALU.pow, [b_st2], [b_st2])
                k.act(xn2[:], xr, AF.Identity, [b_XR[s], b_st2], [b_xn2], scale=st2[:, 2:3])

            def norm_b(tix, j):
                b, ti = tiles[tix]
                for kk in range(8):
                    k.tr(TR2[:, kk, :], xn2[:, kk * 128:(kk + 1) * 128], identb[:], [b_xn2], [b_tr2])
                for kk in range(8):
                    k.ts("dve", hT2[:, kk, j * 128:(j + 1) * 128], TR2[:, kk, :], G2[:, kk, b:b + 1], SH2[:, kk, b:b + 1],
                         ALU.mult, ALU.add, [b_tr2], [b_hT2[j]])

            def norm_sub(tix, j):
                norm_a(tix, j)
                norm_b(tix, j)

            b_bank7 = Buf("bk7_2", excl=True)
            Y2sets = [((banks[7][:], banks[1][:]), (b_bank7, b_PG[0])), ((Y2[0], Y2[1]), (b_Y2[0], b_Y2[1]))]
            yst2d = sb2("yst2d", [128, 2, 8])
            b_yst2d = [B("yst2d0"), B("yst2d1")]

            for j in range(4):
                norm_sub(0, j)
            load_rest_weights()
            for tix, (b, ti) in enumerate(tiles):
                if ti == 0:
                    k.dma("sp", GP2[:], gsc[1, b:b + 1, :].partition_broadcast(128), (), [b_GP2])
                    k.memset("pool", halo2[:], 0.0, [b_halo2])
                for c in range(NCH):
                    s = c % 2
                    for kk in range(8):
                        k.mm(PG[s], Wup[:, kk, c * 128:(c + 1) * 128], hT2[:, kk, :], kk == 0, kk == 7,
                             b_hT2 + [b_Wup[0][c // 4]], [b_PG[s]])
                    for kk in range(8):
                        k.mm(PV[s], Wup[:, kk, DFF + c * 128: DFF + (c + 1) * 128], hT2[:, kk, :], kk == 0, kk == 7,
                             b_hT2 + [b_Wup[1][c // 4]], [b_PV[s]])
                    for (Ux, bU, ps, bps, ac, bac, ci) in ((Ug, b_Ug, PG[s], b_PG[s], accg, b_accg, c),
                                                           (Uv, b_Uv, PV[s], b_PV[s], accv, b_accv, NCH + c)):
                        k.cp("act", Ux[:, 2:514], ps, [bps], [bU])
                        k.cp("pool", Ux[:, 0:2], halo2[:, ci, :], [b_halo2], [bU])
                        k.cp("pool", halo2[:, ci, :], Ux[:, 512:514], [bU], [b_halo2])
                        k.act(ac[:], ps, AF.Identity, [bps] + RC, [bac], scale=ffw[:, ci, 2:3], bias=ffb[:, ci:ci + 1])
                        for tap in (1, 0):
                            k.stt(ac[:], Ux[:, tap:tap + 512], ffw[:, ci, tap:tap + 1], ac[:], ALU.mult, ALU.add,
                                  [bU, bac] + RC, [bac])
                    if tix == 0 and c < NCH // 2:
                        load_wdn(2 * c)
                        load_wdn(2 * c + 1)
                    k.act(gl[:], accg[:], AF.Gelu_apprx_tanh, [b_accg], [b_gl])
                    k.tt("dve", aT[:, c, :], gl[:], accv[:], ALU.mult, [b_gl, b_accv], [b_aT[c]])
                for j in range(4):
                    t0 = b * S + ti * 512 + j * 128
                    s = (tix * 4 + j) % 2
                    (Ya, Yb), (bYa, bYb) = Y2sets[j % 2]
                    Yh, bYh = (Ya, Yb), (bYa, bYb)
                    ys, bys = yst2d[:, j % 2, :], b_yst2d[j % 2]
                    if (tix, j) not in xo_loaded:
                        load_xo(tix, j)
                    if j + 1 < 4:
                        load_xo(tix, j + 1)
                        if tix + 1 < len(tiles):
                            load_xr(tix + 1, j + 1)
                    if tix + 1 < len(tiles):
                        norm_a(tix + 1, j)
                    for h in range(2):
                        for c in range(NCH):
                            k.mm(Yh[h], aT[:, c, j * 128:(j + 1) * 128], Wdn[:, c, h * 512:(h + 1) * 512], c == 0, c == NCH - 1,
                                 [b_aT[c], b_Wdn[c]], [bYh[h]])
                    if tix + 1 < len(tiles):
                        norm_b(tix + 1, j)
                    for h in range(2):
                        k.act(junk2[:, h * 512:(h + 1) * 512], Yh[h], AF.Square, [bYh[h]], [b_junk2, bys],
                              accum=ys[:, h:h + 1])
                    k.tt("dve", ys[:, 2:3], ys[:, 0:1], ys[:, 1:2], ALU.add, [bys], [bys])
                    k.ts("dve", ys[:, 3:4], ys[:, 2:3], 1.0 / D, EPS, ALU.mult, ALU.add, [bys], [bys])
                    k.tt("pool", ys[:, 4:5], ys[:, 3:4], neghalf[:, 0:1], ALU.pow, [bys], [bys])
                    xo = XO[:, s, :]
                    for h in range(2):
                        k.stt(Yh[h], Yh[h], ys[:, 4:5], GP2[:, 0, h * 512:(h + 1) * 512], ALU.mult, ALU.mult,
                              [bYh[h], bys, b_GP2], [bYh[h]])
                        k.tt("dve", xo[:, h * 512:(h + 1) * 512], Yh[h], xo[:, h * 512:(h + 1) * 512], ALU.add,
                             [bYh[h], b_XO[s]], [b_XO[s]])
                    bo = B(f"o{t0}")
                    out_bufs.append(bo)
                    k.dma("sp", out_d[t0:t0 + 128, :], xo, [b_XO[s]], [bo])
            k.fence("sp", out_bufs)
            with ExitStack() as sems, nc.Block() as block:
                P.emit(block, sems)
            stats2 = P.stats
    nc._stats = (stats1, stats2)
    return nc


def _consts(NSEQ, S):
    NB = S // 128
    bf = ml_dtypes.bfloat16
    p = np.arange(128)
    identb = np.eye(128, dtype=np.float32).astype(bf)
    triu = (p[:, None] <= p[None, :]).astype(np.float32)
    masku = triu * np.float32(KSCALE)
    mc = np.where(p[:, None] <= p[None, :], 0.0, NEG).astype(np.float32)
    mp = np.where(p[:, None] > p[None, :], 0.0, NEG).astype(np.float32)
    maskc = np.tile(mc, (1, 4)).astype(bf)
    maskp = np.tile(mp, (1, 4)).astype(bf)
    half = 32
    inv = (10000.0 ** (-np.arange(half, dtype=np.float32) / half)).astype(np.float32)
    pos = (np.arange(NB)[None, :] * 128 + p[:, None]).astype(np.float32)
    ang = pos[:, :, None] * inv[None, None, :]
    return dict(identb=identb, triu=triu, masku=masku, maskc=maskc, maskp=maskp,
                cos=np.cos(ang).astype(np.float32), sin=np.sin(ang).astype(np.float32),
                i4=np.eye(4, dtype=np.float32))


def _fp(v, nk):
    return np.ascontiguousarray(np.asarray(v, np.float32).reshape(nk, 128).T)


def _rb(v):
    v = np.asarray(v, np.float32).reshape(1, -1)
    return np.ascontiguousarray(np.broadcast_to(v, (128, v.shape[1])))


_NC_CACHE = {}


def run(inputs, NSEQ, S):
    f = lambda a: np.ascontiguousarray(np.asarray(a, np.float32))
    x = f(inputs["x"])
    c = f(inputs["c"])
    assert x.shape[0] == NCORES * NSEQ and x.shape[1] == S
    key = (NSEQ, S)
    if key not in _NC_CACHE:
        _NC_CACHE[key] = build(NSEQ, S)
    nc = _NC_CACHE[key]
    cst = _consts(NSEQ, S)
    shared = dict(cst)
    shared["w_ada"] = f(inputs["w_ada"][0])
    shared["b_ada"] = f(inputs["b_ada"][0]).reshape(1, -1)
    shared["w_in"] = f(inputs["w_in"][0])
    shared["w_out"] = f(inputs["w_out"][0])
    shared["w_up"] = f(inputs["w_up"][0])
    shared["w_down"] = f(inputs["w_down"][0])
    shared["pmg"] = _fp(inputs["pre_mix_g"][0], 8)
    shared["pfg"] = _fp(inputs["pre_ffn_g"][0], 8)
    post = np.stack([f(inputs["post_mix_g"][0]), f(inputs["post_ffn_g"][0])], 0)
    shared["post4"] = np.ascontiguousarray(np.broadcast_to(post[None], (NSEQ, 2, D)))
    shared["mlcw"] = np.ascontiguousarray(f(inputs["ml_conv_w"][0]).reshape(4, 8, 128).transpose(2, 1, 0))
    shared["mlcb"] = _fp(inputs["ml_conv_b"][0], 8)
    shared["ib16"] = _rb(np.tile(f(inputs["ml_i_b"][0]), 4))
    shared["fb16"] = _rb(np.tile(f(inputs["ml_f_b"][0]), 4))
    shared["mlg"] = _rb(f(inputs["ml_norm_g"][0]).reshape(-1))
    shared["sinks"] = _rb(inputs["attn_sinks"][0])
    shared["attg"] = _rb(inputs["attn_norm_g"][0])
    shared["ffw"] = np.ascontiguousarray(f(inputs["ffn_conv_w"][0]).reshape(3, 2 * NCH, 128).transpose(2, 1, 0))
    shared["ffb"] = _fp(inputs["ffn_conv_b"][0], 2 * NCH)
    in_maps = []
    for i in range(NCORES):
        m = dict(shared)
        m["x"] = np.ascontiguousarray(x[i * NSEQ:(i + 1) * NSEQ].reshape(NSEQ * S, D))
        ci = c[i * NSEQ:(i + 1) * NSEQ]
        m["cT"] = np.ascontiguousarray(ci.reshape(NSEQ, 8, 128).transpose(2, 1, 0))
        in_maps.append(m)
    res = run_bass_kernel_spmd(nc, in_maps, core_ids=list(range(NCORES)))
    outs = [np.asarray(r["out"], np.float32).reshape(NSEQ, S, D) for r in res.results]
    return np.concatenate(outs, axis=0)


def kernel(**inputs):
    x = inputs["x"]
    return run(inputs, x.shape[0] // NCORES, x.shape[1])
```

```python
import math
from contextlib import ExitStack

import numpy as np
import ml_dtypes

import concourse.bass as bass
import concourse.mybir as mybir
from concourse.bass_utils import run_bass_kernel_spmd

F32 = mybir.dt.float32
BF16 = mybir.dt.bfloat16
AF = mybir.ActivationFunctionType
ALU = mybir.AluOpType

NCORES = 8
D = 1024
DFF = 2816
NCH = DFF // 128
INC = 2824
EPS = 1e-6
KSCALE = 1.0 / math.sqrt(128.0)
NEG = -30000.0

SEM_LIMIT = 24000
ENGS = ("pe", "act", "dve", "pool", "sp")


class Buf:
    __slots__ = ("name", "last_w", "readers", "excl")

    def __init__(self, name, excl=False):
        self.name = name
        self.last_w = None
        self.readers = []
        self.excl = excl


class Op:
    __slots__ = ("eng", "fn", "deps", "is_dma", "needed", "waits", "tok")

    def __init__(self, eng, fn, deps, is_dma):
        self.eng = eng
        self.fn = fn
        self.deps = deps
        self.is_dma = is_dma
        self.needed = False
        self.waits = []
        self.tok = None


class Prog:
    def __init__(self, nc, n_dma_sems=12):
        self.nc = nc
        self.ops = []
        self.n_dma_sems = n_dma_sems

    def op(self, eng, fn, reads=(), writes=(), dma=False):
        ex = [b for b in reads if b.excl]
        if ex:
            writes = list(writes) + [b for b in ex if b not in writes]
            reads = [b for b in reads if not b.excl]
        deps = set()
        for b in reads:
            if b.last_w is not None:
                deps.add(b.last_w)
        for b in writes:
            if b.last_w is not None:
                deps.add(b.last_w)
            deps.update(b.readers)
        oid = len(self.ops)
        self.ops.append(Op(eng, fn, deps, dma))
        for b in writes:
            b.last_w = oid
            b.readers = []
        for b in reads:
            if b not in writes:
                b.readers.append(oid)
        return oid

    def emit(self, block, sems_cm):
        nc = self.nc
        ops = self.ops
        n = len(ops)
        eng_ops = {e: [] for e in ENGS}
        for i, o in enumerate(ops):
            eng_ops[o.eng].append(i)
        pos = {}
        for e in ENGS:
            k = 0
            for i in eng_ops[e]:
                if not ops[i].is_dma:
                    pos[i] = k
                    k += 1
        vc = [None] * n
        known = {e: {x: -1 for x in ENGS} for e in ENGS}
        known_dma = {e: set() for e in ENGS}
        for i, o in enumerate(ops):
            kc = known[o.eng]
            kd = known_dma[o.eng]
            for d in sorted(o.deps):
                od = ops[d]
                if od.is_dma:
                    if d in kd:
                        continue
                    o.waits.append(d)
                    kd.add(d)
                else:
                    if od.eng == o.eng and o.eng == "pe":
                        continue
                    if kc[od.eng] >= pos[d]:
                        continue
                    o.waits.append(d)
                    od.needed = True
                dc = vc[d]
                for x in ENGS:
                    if dc[x] > kc[x]:
                        kc[x] = dc[x]
            c = dict(kc)
            if not o.is_dma:
                if pos[i] > c[o.eng]:
                    c[o.eng] = pos[i]
            vc[i] = c

        def new_sem(name):
            return sems_cm.enter_context(nc.semaphore(name))

        for e in ENGS:
            cur = None
            cnt = 0
            k = 0
            for i in eng_ops[e]:
                o = ops[i]
                if o.is_dma or not o.needed:
                    continue
                if cur is None or cnt >= SEM_LIMIT:
                    cur = new_sem(f"s_{e}_{k}_{id(self) % 997}")
                    k += 1
                    cnt = 0
                cnt += 1
                o.tok = (cur, cnt)
        caps = {"sp": 10, "pool": 12, "act": 4, "dve": 2, "pe": 2}
        for e in ENGS:
            dl = [i for i in eng_ops[e] if ops[i].is_dma]
            if not dl:
                continue
            nsem = min(len(dl), caps[e])
            dsems = [new_sem(f"s_dma_{e}_{j}_{id(self) % 997}") for j in range(nsem)]
            dcnt = [0] * nsem
            dprev = [None] * nsem
            for j, i in enumerate(dl):
                o = ops[i]
                s = j % nsem
                if dprev[s] is not None and dprev[s] not in o.waits:
                    o.waits.append(dprev[s])
                dcnt[s] += 16
                o.tok = (dsems[s], dcnt[s])
                dprev[s] = i
        self.stats = {e: len(eng_ops[e]) for e in ENGS}
        self.stats["waits"] = sum(len(o.waits) for o in ops)
        self.stats["incs"] = sum(1 for o in ops if o.tok is not None)

        def run_engine(e, eng):
            for i in eng_ops[e]:
                o = ops[i]
                seen = {}
                for d in o.waits:
                    sem, val = ops[d].tok
                    key = id(sem)
                    if key not in seen or seen[key][1] < val:
                        seen[key] = (sem, val)
                for sem, val in seen.values():
                    eng.wait_ge(sem, val)
                ins = o.fn(eng)
                if o.tok is not None and ins is not None:
                    ins.then_inc(o.tok[0], 16 if o.is_dma else 1)

        @block.tensor
        def _(eng):
            run_engine("pe", eng)

        @block.scalar
        def _(eng):
            run_engine("act", eng)

        @block.vector
        def _(eng):
            run_engine("dve", eng)

        @block.gpsimd
        def _(eng):
            run_engine("pool", eng)

        @block.sync
        def _(eng):
            run_engine("sp", eng)


class K:
    def __init__(self, P):
        self.P = P

    def mm(self, out, lhsT, rhs, start, stop, R, W):
        self.P.op("pe", lambda e: e.matmul(out, lhsT=lhsT, rhs=rhs, start=start, stop=stop), R, W)

    def tr(self, out, in_, ident, R, W):
        self.P.op("pe", lambda e: e.transpose(out, in_, ident), R, W)

    def act(self, out, in_, func, R, W, scale=1.0, bias=None, accum=None):
        def fn(e):
            kw = {}
            if bias is not None:
                kw["bias"] = bias
            if accum is not None:
                kw["accum_out"] = accum
            return e.activation(out=out, in_=in_, func=func, scale=scale, **kw)
        self.P.op("act", fn, R, W)

    def ts(self, eng, out, in0, s1, s2, op0, op1, R, W):
        def fn(e):
            if op1 is None:
                return e.tensor_scalar(out=out, in0=in0, scalar1=s1, scalar2=None, op0=op0)
            return e.tensor_scalar(out=out, in0=in0, scalar1=s1, scalar2=s2, op0=op0, op1=op1)
        self.P.op(eng, fn, R, W)

    def stt(self, out, in0, scalar, in1, op0, op1, R, W):
        self.P.op("dve", lambda e: e.scalar_tensor_tensor(out=out, in0=in0, scalar=scalar, in1=in1,
                                                          op0=op0, op1=op1), R, W)

    def tt(self, eng, out, in0, in1, op, R, W):
        self.P.op(eng, lambda e: e.tensor_tensor(out=out, in0=in0, in1=in1, op=op), R, W)

    def cp(self, eng, out, in_, R, W):
        if eng == "act":
            self.P.op("act", lambda e: e.activation(out=out, in_=in_, func=AF.Copy), R, W)
        else:
            self.P.op(eng, lambda e: e.tensor_copy(out=out, in_=in_), R, W)

    def recip(self, out, in_, R, W):
        self.P.op("dve", lambda e: e.reciprocal(out=out, in_=in_), R, W)

    def memset(self, eng, ap, val, W):
        self.P.op(eng, lambda e: e.memset(ap, val), (), W)

    def dma(self, eng, out, in_, R, W):
        self.P.op(eng, lambda e: e.dma_start(out=out, in_=in_), R, W, dma=True)

    def fence(self, eng, R):
        self.P.op(eng, lambda e: None, R, ())


def build(NSEQ, S):
    NT = NSEQ * S
    NB = S // 128
    TPS = S // 512
    nc = bass.Bass("TRN2", target_bir_lowering=False)

    def din(name, shape, dt=F32):
        return nc.dram_tensor(name, list(shape), dt, kind="ExternalInput").ap()

    x_d = din("x", [NT, D])
    cT_d = din("cT", [128, 8, NSEQ])
    wada_d = din("w_ada", [D, 6 * D])
    bada_d = din("b_ada", [1, 6 * D])
    win_d = din("w_in", [D, INC])
    wout_d = din("w_out", [D, D])
    wup_d = din("w_up", [D, 2 * DFF])
    wdn_d = din("w_down", [DFF, D])
    pmg_d = din("pmg", [128, 8])
    pfg_d = din("pfg", [128, 8])
    post4_d = din("post4", [NSEQ, 2, D])
    mlcw_d = din("mlcw", [128, 8, 4])
    mlcb_d = din("mlcb", [128, 8])
    ib_d = din("ib16", [128, 16])
    fb_d = din("fb16", [128, 16])
    mlg_d = din("mlg", [128, 512])
    sink_d = din("sinks", [128, 8])
    attg_d = din("attg", [128, 512])
    ffw_d = din("ffw", [128, 2 * NCH, 3])
    ffb_d = din("ffb", [128, 2 * NCH])
    identb_d = din("identb", [128, 128], BF16)
    tri_d = din("triu", [128, 128])
    maskU_d = din("masku", [128, 128])
    maskC_d = din("maskc", [128, 512], BF16)
    maskP_d = din("maskp", [128, 512], BF16)
    cos_d = din("cos", [128, NB, 32])
    sin_d = din("sin", [128, NB, 32])
    i4_d = din("i4", [4, 4])
    out_d = nc.dram_tensor("out", [NT, D], F32, kind="ExternalOutput").ap()
    x1s = nc.dram_tensor("x1s", [NT, D], F32).ap()
    gsc = nc.dram_tensor("gsc", [2, NSEQ, D], F32).ap()

    with ExitStack() as outer:
        def sb(name, shape, dt=F32):
            return outer.enter_context(nc.sbuf_tensor("s_" + name, list(shape), dt))

        G2 = sb("G2", [128, 8, NSEQ])
        SH2 = sb("SH2", [128, 8, NSEQ])
        identb = sb("identb", [128, 128], BF16)
        neghalf = sb("neghalf", [128, 16])
        banks = [outer.enter_context(nc.psum_tensor(f"bank{i}", [128, 512], F32)) for i in range(8)]

        with ExitStack() as ph:
            def sb1(name, shape, dt=F32):
                return ph.enter_context(nc.sbuf_tensor("t_" + name, list(shape), dt))

            P = Prog(nc)
            k = K(P)
            B = lambda n: Buf(n)

            Win = sb1("Win", [128, 8, INC], BF16)
            Wout = sb1("Wout", [128, 8, D], BF16)
            b_Win = [B(f"Win{i}") for i in range(8)]
            b_Wout = [B(f"Wout{i}") for i in range(8)]
            for i in range(8):
                k.dma("pool", Win[:, i, :], win_d[i * 128:(i + 1) * 128, :], (), [b_Win[i]])
            for i in range(8):
                k.dma("pool", Wout[:, i, :], wout_d[i * 128:(i + 1) * 128, :], (), [b_Wout[i]])

            RC = []
            b_cT = B("c_cT")
            cT = sb1("cT", [128, 8, NSEQ])
            k.dma("sp", cT[:], cT_d, (), [b_cT])
            b_i4 = B("c_i4")
            i4 = sb1("i4", [4, 4])
            k.dma("sp", i4[:], i4_d, (), [b_i4])
            onesf = sb1("onesf", [128, 128])
            onesb = sb1("onesb", [128, 2], BF16)
            b_c2 = B("c_misc")
            k.memset("dve", onesf[:], 1.0, [b_c2])
            k.memset("dve", onesb[:], 1.0, [b_c2])
            k.memset("dve", neghalf[:], -0.5, [b_c2])
            b_bk = [Buf(f"bk{i}", excl=True) for i in range(8)]
            RC0 = [b_cT, b_i4, b_c2]
            consts = {}

            def load_consts():
                b_id = B("c_ident")
                RC.append(b_id)
                k.dma("sp", identb[:], identb_d, (), [b_id])
                for (nm, src, shp, dt) in ():
                    pass
            const_srcs = [(nm, src) for (nm, src, shp, dt) in (("pmg", pmg_d, [128, 8], F32), ("pfg", pfg_d, [128, 8], F32),
                                           ("mlcw", mlcw_d, [128, 8, 4], F32), ("mlcb", mlcb_d, [128, 8], F32),
                                           ("ib16", ib_d, [128, 16], F32), ("fb16", fb_d, [128, 16], F32),
                                           ("mlgh", mlg_d, [128, 512], F32), ("esink", sink_d, [128, 8], F32),
                                           ("attg", attg_d, [128, 512], F32), ("triu", tri_d, [128, 128], F32),
                                           ("masku", maskU_d, [128, 128], F32), ("maskc", maskC_d, [128, 512], BF16),
                                           ("maskp", maskP_d, [128, 512], BF16), ("cosT", cos_d, [128, NB, 32], F32),
                                           ("sinT", sin_d, [128, NB, 32], F32))]

            def load_some_consts(cnt):
                for _ in range(cnt):
                    if const_srcs:
                        nm, src = const_srcs.pop(0)
                        k.dma("sp", consts[nm][:], src, (), [consts["b_" + nm]])

            for (nm, shp, dt) in (("pmg", [128, 8], F32), ("pfg", [128, 8], F32), ("mlcw", [128, 8, 4], F32),
                                  ("mlcb", [128, 8], F32), ("ib16", [128, 16], F32), ("fb16", [128, 16], F32),
                                  ("mlgh", [128, 512], F32), ("esink", [128, 8], F32), ("attg", [128, 512], F32),
                                  ("triu", [128, 128], F32), ("masku", [128, 128], F32), ("maskc", [128, 512], BF16),
                                  ("maskp", [128, 512], BF16), ("cosT", [128, NB, 32], F32), ("sinT", [128, NB, 32], F32)):
                consts[nm] = sb1(nm, shp, dt)
                consts["b_" + nm] = B("c_" + nm)
                RC.append(consts["b_" + nm])
            pmg, pfg, mlcw, mlcb, ib16, fb16 = (consts[n_] for n_ in ("pmg", "pfg", "mlcw", "mlcb", "ib16", "fb16"))
            mlgh, esink, attg, triu, masku = (consts[n_] for n_ in ("mlgh", "esink", "attg", "triu", "masku"))
            maskc, maskp, cosT, sinT = (consts[n_] for n_ in ("maskc", "maskp", "cosT", "sinT"))
            RC += RC0
            sT = sb1("sT", [128, 8, NSEQ])
            b_sT = B("sT")
            k.act(sT[:], cT[:], AF.Tanh, RC0, [b_sT], scale=0.5)
            k.ts("dve", sT[:], sT[:], 0.5, 0.5, ALU.mult, ALU.add, [b_sT], [b_sT])
            k.tt("dve", sT[:], sT[:], cT[:], ALU.mult, [b_sT] + RC0, [b_sT])
            modc = sb1("modc", [NSEQ, 1, 512])
            b_mc = B("modc")
            b_modc = [b_mc, b_mc]
            badar = sb1("badar", [1, 512])
            postc = sb1("postc", [NSEQ, 512])
            wring = sb1("wring", [128, 2, 2, 512])
            b_wr = [B("wr0"), B("wr1")]
            b_bd = B("bd")
            b_pc = B("pc")
            pmt = banks[3][:, 0:32 * NSEQ].rearrange("p (c b) -> p c b", b=NSEQ)
            b_gsc = [[B(f"gsc{i}_{q}") for q in range(2)] for i in range(2)]
            grp = {0: 0, 1: 1, 3: 2, 4: 3}
            def ada_chunk(n):
                m_ = n % 2
                c0 = n * 512
                g1024 = n // 2
                pm = banks[1 + m_][0:NSEQ, :]
                k.dma("sp", badar[:], bada_d[0:1, c0:c0 + 512], (), [b_bd])
                for kp in range(4):
                    s_ = (n * 4 + kp) % 2
                    k.dma("sp", wring[:, s_, :, :],
                          wada_d[kp * 256:(kp + 1) * 256, c0:c0 + 512].rearrange("(k p) n -> p k n", p=128), (), [b_wr[s_]])
                    for kk in range(2):
                        k.mm(pm, sT[:, kp * 2 + kk, :], wring[:, s_, kk, :], kp == 0 and kk == 0, False,
                             [b_sT, b_wr[s_]], [b_bk[1 + m_]])
                if n == 0:
                    load_consts()
                load_some_consts(4)
                k.mm(pm, onesf[0:1, 0:NSEQ], badar[0:1, :], False, True, RC0 + [b_bd], [b_bk[1 + m_]])
                k.cp("act", modc[:, 0, :], pm, [b_bk[1 + m_]], [b_modc[m_]])
                if g1024 in grp:
                    for c in range(4):
                        k.mm(pmt[:, grp[g1024] * 8 + m_ * 4 + c, :], modc[:, 0, c * 128:(c + 1) * 128],
                             i4[0:NSEQ, 0:NSEQ], True, True, [b_modc[m_]] + RC0, [b_bk[3]])
                else:
                    gi = 0 if g1024 == 2 else 1
                    k.dma("sp", postc[:], post4_d[:, gi, m_ * 512:(m_ + 1) * 512], (), [b_pc])
                    k.tt("dve", modc[:, 0, :], modc[:, 0, :], postc[:], ALU.mult, [b_modc[m_], b_pc], [b_modc[m_]])
                    k.dma("sp", gsc[gi][:, m_ * 512:(m_ + 1) * 512], modc[:, 0, :], [b_modc[m_]], [b_gsc[gi][m_]])
            for n in range(4):
                ada_chunk(n)
            b_c3 = B("c_esink2")
            k.ts("dve", mlgh[:], mlgh[:], 0.5, None, ALU.mult, None, [consts["b_mlgh"]], [consts["b_mlgh"]])
            k.act(esink[:], esink[:], AF.Exp, [consts["b_esink"]], [consts["b_esink"]])
            G1 = sb1("G1", [128, 8, NSEQ])
            SH1 = sb1("SH1", [128, 8, NSEQ])
            b_G = B("G")
            b_G2 = B("G2")
            k.cp("dve", SH1[:], pmt[:, 0:8, :], [b_bk[3]], [b_G])
            k.stt(G1[:], pmt[:, 8:16, :], 1.0, pmg[:].unsqueeze(2).broadcast_to([128, 8, NSEQ]),
                  ALU.add, ALU.mult, [b_bk[3]] + RC, [b_G])

            def ada_finish():
                k.cp("dve", SH2[:], pmt[:, 16:24, :], [b_bk[3]], [b_G2])
                k.stt(G2[:], pmt[:, 24:32, :], 1.0, pfg[:].unsqueeze(2).broadcast_to([128, 8, NSEQ]),
                      ALU.add, ALU.mult, [b_bk[3]] + RC, [b_G2])

            XB = sb1("XB", [128, 8, D])
            b_XB = [B(f"XB{i}") for i in range(8)]
            GP1 = sb1("GP1", [128, 1, D])
            b_GP1 = B("GP1")
            hT = sb1("hT", [128, 8, 512], BF16)
            b_hT = [B(f"hT{j}") for j in range(4)]
            qkT = sb1("qkT", [128, 8, 512], BF16)
            b_qkT = [B(f"qkT{c}") for c in range(8)]
            U = sb1("U", [128, 2, 515])
            b_U = [B("U0"), B("U1")]
            acc = sb1("acc", [128, 2, 512])
            b_acc = [B("acc0"), B("acc1")]
            halo = sb1("halo", [128, 8, 3])
            b_halo = B("halo")
            xn = sb1("xn", [128, 2, D], BF16)
            b_xn = [B("xn0"), B("xn1")]
            junk = sb1("junk", [128, D], BF16)
            b_junk = B("junk")
            st = sb1("st", [128, 16])
            b_st = B("st")
            gat = sb1("gat", [128, 8, 16])
            b_gat = B("gat")
            sd = sb1("sd", [128, 16])
            eab = sb1("eab", [128, 16], BF16)
            tanho = sb1("tanho", [128, 512], BF16)
            b_tanho = B("tanho")
            gso = sb1("gso", [128, 4, 512], BF16)
            b_gso = [B(f"gso{j}") for j in range(4)]
            vtil = sb1("vtil", [128, 4, 4, 128], BF16)
            b_vtil = [B(f"vtil{j}") for j in range(4)]
            STm = sb1("STm", [128, 4, 4, 128], BF16)
            b_STm = [B(f"STm{j}") for j in range(4)]
            ktok = sb1("ktok", [128, 4, 4, 128], BF16)
            b_ktok = [B(f"ktok{j}") for j in range(4)]
            CT = sb1("CT", [128, 4, 128])
            CTb = sb1("CTb", [128, 4, 128], BF16)
            nv = sb1("nv", [128, 4])
            nvt = sb1("nvt", [128, 4])
            nvb = sb1("nvb", [128, 4], BF16)
            b_CT = B("CT")
            b_CTb = B("CTb")
            b_nv = B("nv")
            b_nvb = B("nvb")
            mst = sb1("mst", [128, 2, 8, 4])
            b_mst = [B("mst0"), B("mst1")]
            rt1 = sb1("rt1", [128, 640])
            rt2 = sb1("rt2", [128, 640])
            rot = sb1("rot", [128, 4, 640], BF16)
            b_rt1, b_rt2 = B("rt1"), B("rt2")
            b_rot = [B(f"rot{j}") for j in range(4)]
            qTa = sb1("qTa", [128, 4, 4, 128], BF16)
            b_qTa = [B(f"qTa{j}") for j in range(4)]
            kTr = sb1("kTr", [128, 8, 128], BF16)
            b_kTr = [B(f"kTr{i}") for i in range(8)]
            vat = sb1("vat", [128, 8, 128], BF16)
            b_vat = [B(f"vat{i}") for i in range(8)]
            pT = sb1("pT", [128, 4, 512], BF16)
            b_pT = [B(f"pT{i}") for i in range(4)]
            ast = sb1("ast", [128, 2, 32])
            b_ast = [B("ast0"), B("ast1")]
            osb = sb1("osb", [128, 1, 512])
            b_os = B("osb")
            b_osb = [b_os, b_os]
            mix = sb1("mix", [128, 2, D], BF16)
            b_mixm, b_mixa = [B("mixm0"), B("mixm1")], [B("mixa0"), B("mixa1")]
            mixT = sb1("mixT", [128, 2, 8, 128], BF16)
            b_mixT = [B("mixT0"), B("mixT1")]
            yst = sb1("yst", [128, 2, 8])
            b_yst = [B("yst0"), B("yst1")]

            TR = banks[0][:].bitcast(BF16).rearrange("p (k n) -> p k n", k=8)
            b_bk0a = b_bk0b = b_bk[0]
            b_bk1, b_bk2, b_bk3, b_bk4, b_bk5, b_bk7 = b_bk[1], b_bk[2], b_bk[3], b_bk[4], b_bk[5], b_bk[7]
            QK = [banks[1][:], banks[2][:]]
            A0, A1 = banks[3][:], banks[4][:]
            Bt0, Bt1 = banks[5][:], banks[6][:, 0:256]
            b_bt1 = b_bk[6]
            GT = banks[6][:, 256:288].rearrange("p (j g) -> p j g", g=8)
            CSa = banks[6][:, 288:304]
            CSb = banks[6][:, 304:320]
            DN = banks[6][:, 320:324]
            DNs = banks[6][:, 324:328]
            SD = banks[6][:, 328:336]
            RTk = banks[6][:, 336:400].bitcast(BF16)
            b_GT = b_CS = b_DN = b_DNs = b_SD = b_RTk = b_bk[6]
            SC = banks[7][:]
            ST = banks[1][:].rearrange("p (h n) -> p h n", h=4)
            OE = banks[2][:].rearrange("p (h n) -> p h n", h=4)
            DC = banks[3][:].rearrange("p (h n) -> p h n", h=4)
            SO = banks[1][:].rearrange("p (h n) -> p h n", h=8)
            KT = banks[0][:, 0:256].bitcast(BF16).rearrange("p (h n) -> p h n", h=4)
            RTq = banks[0][:, 256:512].bitcast(BF16).rearrange("p (h n) -> p h n", h=4)

            x1_bufs = []

            def load_x_tile(b, ti):
                for j in range(4):
                    slot = ((b * TPS + ti) * 4 + j) % 8
                    t0 = b * S + ti * 512 + j * 128
                    k.dma("sp", XB[:, slot, :], x_d[t0:t0 + 128, :], (), [b_XB[slot]])

            tiles = [(b, ti) for b in range(NSEQ) for ti in range(TPS)]
            load_x_tile(*tiles[0])

            for tix, (b, ti) in enumerate(tiles):
                if tix + 1 < len(tiles):
                    load_x_tile(*tiles[tix + 1])
                if ti == 0:
                    k.memset("pool", CT[:], 0.0, [b_CT])
                    k.memset("pool", CTb[:], 0.0, [b_CTb])
                    k.memset("pool", nv[:], 0.0, [b_nv])
                    k.memset("pool", nvb[:], 0.0, [b_nvb])
                    k.memset("pool", halo[:], 0.0, [b_halo])
                for j in range(4):
                    slot = (tix * 4 + j) % 8
                    first_dep = (b_gsc[0] + b_gsc[1]) if (tix == 0 and j == 0) else []
                    k.act(junk[:], XB[:, slot, :], AF.Square, [b_XB[slot]], [b_junk, b_st], accum=st[:, j:j + 1])
                k.ts("dve", st[:, 4:8], st[:, 0:4], 1.0 / D, EPS, ALU.mult, ALU.add, [b_st], [b_st])
                k.tt("pool", st[:, 8:12], st[:, 4:8], neghalf[:, 0:4], ALU.pow, [b_st] + RC, [b_st])
                TR7 = banks[7][:].bitcast(BF16).rearrange("p (k n) -> p k n", k=8)
                def emit_xn(j):
                    slot = (tix * 4 + j) % 8
                    k.act(xn[:, j % 2, :], XB[:, slot, :], AF.Identity, [b_XB[slot], b_st], [b_xn[j % 2]],
                          scale=st[:, 8 + j:9 + j])

                emit_xn(0)
                for j in range(4):
                    a_ = j % 2
                    TRj, bTR = (TR, [b_bk[0]]) if a_ == 0 else (TR7, [b_bk[7]])
                    for kk in range(8):
                        k.tr(TRj[:, kk, :], xn[:, a_, kk * 128:(kk + 1) * 128], identb[:], [b_xn[a_]] + RC, bTR)
                    if j + 1 < 4:
                        emit_xn(j + 1)
                    for kk in range(8):
                        k.ts("dve", hT[:, kk, j * 128:(j + 1) * 128], TRj[:, kk, :], G1[:, kk, b:b + 1],
                             SH1[:, kk, b:b + 1], ALU.mult, ALU.add, bTR + [b_G], [b_hT[j]])
                for j in range(4):
                    for kk in range(8):
                        k.mm(GT[:, j, :], hT[:, kk, j * 128:(j + 1) * 128], Win[:, kk, 2048:2056], kk == 0, kk == 7,
                             [b_hT[j], b_Win[kk]], [b_GT])
                g3 = lambda i: gat[:, i, :].rearrange("p (j h) -> p j h", h=4)
                k.tt("dve", g3(0), GT[:, :, 4:8], fb16[:].rearrange("p (j h) -> p j h", h=4), ALU.add, [b_GT] + RC, [b_gat])
                k.tt("dve", g3(1), GT[:, :, 0:4], ib16[:].rearrange("p (j h) -> p j h", h=4), ALU.add, [b_GT] + RC, [b_gat])
                k.act(gat[:, 2, :], gat[:, 0, :], AF.Exp, [b_gat], [b_gat], scale=-1.0)
                k.act(gat[:, 3, :], gat[:, 2, :], AF.Ln, [b_gat], [b_gat], bias=1.0)
                for c in range(8):
                    s = c % 2
                    bq = (b_bk1, b_bk2)[s]
                    for kk in range(8):
                        k.mm(QK[s], Win[:, kk, c * 128:(c + 1) * 128], hT[:, kk, :], kk == 0, kk == 7,
                             b_hT + [b_Win[kk]], [bq])
                    k.cp("act", U[:, s, 3:515], QK[s], [bq], [b_U[s]])
                    k.cp("pool", U[:, s, 0:3], halo[:, c, :], [b_halo], [b_U[s]])
                    k.cp("pool", halo[:, c, :], U[:, s, 512:515], [b_U[s]], [b_halo])
                    k.act(acc[:, s, :], QK[s], AF.Identity, [bq] + RC, [b_acc[s]], scale=mlcw[:, c, 3:4], bias=mlcb[:, c:c + 1])
                    for tap in (2, 1, 0):
                        k.stt(acc[:, s, :], U[:, s, tap:tap + 512], mlcw[:, c, tap:tap + 1], acc[:, s, :],
                              ALU.mult, ALU.add, [b_U[s], b_acc[s]] + RC, [b_acc[s]])
                    k.act(qkT[:, c, :], acc[:, s, :], AF.Silu, [b_acc[s]], [b_qkT[c]])
                    if tix == 0:
                        ada_chunk(4 + c)
                if tix == 0:
                    ada_finish()
                if ti == 0:
                    k.dma("sp", GP1[:], gsc[0, b:b + 1, :].partition_broadcast(128), b_gsc[0], [b_GP1])
                k.mm(CSa, triu[:], gat[:, 3, :], True, True, [b_gat] + RC, [b_CS])
                k.mm(CSb, onesf[:], gat[:, 3, :], True, True, [b_gat] + RC, [b_CS])
                k.tt("dve", gat[:, 4, :], gat[:, 1, :], CSa, ALU.add, [b_gat, b_CS], [b_gat])
                k.act(gat[:, 5, :], gat[:, 4, :], AF.Exp, [b_gat], [b_gat])
                k.act(gat[:, 6, :], CSa, AF.Exp, [b_CS], [b_gat])
                k.act(gat[:, 7, :], CSb, AF.Exp, [b_CS], [b_gat], scale=-1.0)
                k.ts("dve", sd[:], gat[:, 7, :], KSCALE, None, ALU.mult, None, [b_gat], [b_gat])
                k.cp("dve", eab[:], gat[:, 5, :], [b_gat], [b_gat])
                ea, eb, dec = gat[:, 5, :], gat[:, 6, :], gat[:, 7, :]

                for j in range(4):
                    blk = ti * 4 + j
                    cols = slice(j * 128, (j + 1) * 128)
                    vs = blk % 8
                    for h in range(2):
                        outp, bb = (A0, b_bk3) if h == 0 else (A1, b_bk4)
                        for kk in range(8):
                            k.mm(outp, hT[:, kk, cols], Win[:, kk, 1024 + h * 512: 1536 + h * 512], kk == 0, kk == 7,
                                 [b_hT[j], b_Win[kk]], [bb])
                    for kk in range(8):
                        k.mm(Bt0, hT[:, kk, cols], Win[:, kk, 2056:2568], kk == 0, kk == 7, [b_hT[j], b_Win[kk]], [b_bk5])
                    for kk in range(8):
                        k.mm(Bt1, hT[:, kk, cols], Win[:, kk, 2568:2824], kk == 0, kk == 7, [b_hT[j], b_Win[kk]], [b_bt1])
                    for h in range(4):
                        k.act(vtil[:, j, h, :], A0[:, h * 128:(h + 1) * 128], AF.Identity, [b_bk3, b_gat], [b_vtil[j]],
                              scale=ea[:, j * 4 + h: j * 4 + h + 1])
                    k.act(tanho[:], A1, AF.Tanh, [b_bk4], [b_tanho], scale=0.5)
                    k.stt(gso[:, j, :], tanho[:], 1.0, mlgh[:], ALU.add, ALU.mult, [b_tanho] + RC, [b_gso[j]])
                    X4 = Bt0[:, 0:512].rearrange("p (h t d) -> p h t d", t=2, d=32)
                    Xk = Bt1[:, 0:128].rearrange("p (h t d) -> p h t d", t=2, d=32)
                    rotj = rot[:, j, :]
                    for (X, nh, off) in ((X4, 8, 0), (Xk, 2, 512)):
                        bsrc = b_bk5 if nh == 8 else b_bt1
                        t1v = rt1[:, off:off + nh * 64].rearrange("p (h t d) -> p h t d", t=2, d=32)
                        t2v = rt2[:, off:off + nh * 64].rearrange("p (h t d) -> p h t d", t=2, d=32)
                        rov = rotj[:, off:off + nh * 64].rearrange("p (h t d) -> p h t d", t=2, d=32)
                        cb = cosT[:, blk, :].unsqueeze(1).unsqueeze(1).broadcast_to([128, nh, 2, 32])
                        sb_ = sinT[:, blk, :].unsqueeze(1).broadcast_to([128, nh, 32])
                        k.tt("dve", t1v, X, cb, ALU.mult, [bsrc] + RC, [b_rt1])
                        k.tt("dve", t2v[:, :, 0, :], X[:, :, 1, :], sb_, ALU.mult, [bsrc] + RC, [b_rt2])
                        k.tt("dve", t2v[:, :, 1, :], X[:, :, 0, :], sb_, ALU.mult, [bsrc] + RC, [b_rt2])
                        if nh == 8:
                            pv = lambda t, i: t[:, off:off + 512].rearrange("p (g r t d) -> p g r t d", g=2, r=4, t=2)[:, :, :, i, :]
                            po = lambda i: rotj[:, 0:512].rearrange("p (r g t d) -> p g r t d", g=2, r=4, t=2)[:, :, :, i, :]
                            k.tt("pool", po(0), pv(rt1, 0), pv(rt2, 0), ALU.subtract, [b_rt1, b_rt2], [b_rot[j]])
                            k.tt("pool", po(1), pv(rt1, 1), pv(rt2, 1), ALU.add, [b_rt1, b_rt2], [b_rot[j]])
                        else:
                            k.tt("pool", rov[:, :, 0, :], t1v[:, :, 0, :], t2v[:, :, 0, :], ALU.subtract, [b_rt1, b_rt2], [b_rot[j]])
                            k.tt("pool", rov[:, :, 1, :], t1v[:, :, 1, :], t2v[:, :, 1, :], ALU.add, [b_rt1, b_rt2], [b_rot[j]])
                    k.cp("act", vat[:, vs, :], Bt1[:, 128:256], [b_bt1], [b_vat[vs]])
                for j in range(4):
                    blk = ti * 4 + j
                    cols = slice(j * 128, (j + 1) * 128)
                    vs = blk % 8
                    if j % 2 == 0:
                        STv, bST = banks[1][:].rearrange("p (h n) -> p h n", h=4), [b_bk1]
                        KTv, bKT = KT, [b_bk0a]
                        RTv, bRT = RTq, [b_bk0b]
                    else:
                        STv, bST = banks[2][:].rearrange("p (h n) -> p h n", h=4), [b_bk2]
                        KTv = banks[7][:, 0:256].bitcast(BF16).rearrange("p (h n) -> p h n", h=4)
                        RTv = banks[7][:, 256:512].bitcast(BF16).rearrange("p (h n) -> p h n", h=4)
                        bKT = bRT = [b_bk7]
                    for h in range(4):
                        k.mm(STv[:, h, :], qkT[:, 4 + h, cols], qkT[:, h, cols], True, True, [b_qkT[4 + h], b_qkT[h]], bST)
                    k.tt("dve", STm[:, j, :, :], STv, masku[:].unsqueeze(1).broadcast_to([128, 4, 128]), ALU.mult,
                         bST + RC, [b_STm[j]])
                    for h in range(4):
                        k.tr(KTv[:, h, :], qkT[:, 4 + h, cols], identb[:], [b_qkT[4 + h]] + RC, bKT)
                    k.cp("act", ktok[:, j, :, :], KTv, bKT, [b_ktok[j]])
                    for pr in range(4):
                        k.tr(RTv[:, pr, :], rot[:, j, pr * 128:(pr + 1) * 128], identb[:], [b_rot[j]] + RC, bRT)
                    k.tr(RTk, rot[:, j, 512:640], identb[:], [b_rot[j]] + RC, [b_RTk])
                    k.cp("act", qTa[:, j, :, :], RTv, bRT, [b_qTa[j]])
                    k.cp("act", kTr[:, vs, :], RTk, [b_RTk], [b_kTr[vs]])

                def o_transposes(j):
                    a = j % 2
                    for kk in range(8):
                        k.tr(TR[:, kk, :], mix[:, a, kk * 128:(kk + 1) * 128], identb[:], [b_mixm[a], b_mixa[a]] + RC,
                             [b_bk0a, b_bk0b])
                    k.cp("act", mixT[:, a, :, :], TR, [b_bk0a, b_bk0b], [b_mixT[a]])

                def o_rest(j):
                    a = j % 2
                    blk = ti * 4 + j
                    slot = (tix * 4 + j) % 8
                    for h in range(2):
                        outp, bb = (A1, b_bk4) if h == 0 else (Bt0, b_bk5)
                        for kk in range(8):
                            k.mm(outp, mixT[:, a, kk, :], Wout[:, kk, h * 512:(h + 1) * 512], kk == 0, kk == 7,
                                 [b_mixT[a], b_Wout[kk]], [bb])
                    ys = yst[:, a, :]
                    k.act(junk[:, 0:512], A1, AF.Square, [b_bk4], [b_junk, b_yst[a]], accum=ys[:, 0:1])
                    k.act(junk[:, 512:1024], Bt0, AF.Square, [b_bk5], [b_junk, b_yst[a]], accum=ys[:, 1:2])
                    k.tt("dve", ys[:, 2:3], ys[:, 0:1], ys[:, 1:2], ALU.add, [b_yst[a]], [b_yst[a]])
                    k.ts("dve", ys[:, 3:4], ys[:, 2:3], 1.0 / D, EPS, ALU.mult, ALU.add, [b_yst[a]], [b_yst[a]])
                    k.tt("pool", ys[:, 4:5], ys[:, 3:4], neghalf[:, 0:1], ALU.pow, [b_yst[a]] + RC, [b_yst[a]])
                    xb = XB[:, slot, :]
                    for h, (outp, bb) in enumerate(((A1, b_bk4), (Bt0, b_bk5))):
                        k.stt(outp, outp, ys[:, 4:5], GP1[:, 0, h * 512:(h + 1) * 512], ALU.mult, ALU.mult,
                              [bb, b_yst[a], b_GP1], [bb])
                        k.tt("dve", xb[:, h * 512:(h + 1) * 512], outp, xb[:, h * 512:(h + 1) * 512], ALU.add,
                             [bb, b_XB[slot]], [b_XB[slot]])
                    t0 = b * S + blk * 128
                    bx1 = B(f"x1_{t0}")
                    x1_bufs.append(bx1)
                    k.dma("sp", x1s[t0:t0 + 128, :], xb, [b_XB[slot]], [bx1])

                for j in range(4):
                    blk = ti * 4 + j
                    cols = slice(j * 128, (j + 1) * 128)
                    vs, pvs = blk % 8, (blk - 1) % 8
                    a = j % 2
                    kbs = ([(pvs, maskp)] if blk > 0 else []) + [(vs, maskc)]
                    scl = [(g, kb, msk) for g in range(2) for (kb, msk) in kbs]
                    pidx = {(g, kb): i for i, (g, kb, _) in enumerate(scl)}

                    def emit_sc(i):
                        if i >= len(scl):
                            return
                        g, kb, msk = scl[i]
                        k.mm(SC, kTr[g * 64:(g + 1) * 64, kb, :],
                             qTa[g * 64:(g + 1) * 64, j, :, :].rearrange("p h n -> p (h n)"), True, False,
                             [b_kTr[kb], b_qTa[j]], [b_bk7])
                        k.mm(SC, identb[:], msk[:], False, True, RC, [b_bk7])
                        k.act(pT[:, i, :], SC, AF.Exp, [b_bk7], [b_pT[i]], scale=0.125)

                    emit_sc(0)
                    for h in range(4):
                        k.mm(DC[:, h, :], ktok[:, j, h, :], vtil[:, j, h, :], True, True, [b_ktok[j], b_vtil[j]], [b_bk3])
                        k.mm(DNs[:, h:h + 1], ktok[:, j, h, :], eab[:, j * 4 + h: j * 4 + h + 1], True, True,
                             [b_ktok[j], b_gat], [b_DNs])
                    emit_sc(1)
                    for h in range(4):
                        k.mm(OE[:, h, :], qkT[:, h, cols], CTb[:, h, :], True, False, [b_qkT[h], b_CTb], [b_bk2])
                        k.mm(OE[:, h, :], STm[:, j, h, :], vtil[:, j, h, :], False, True, [b_STm[j], b_vtil[j]], [b_bk2])
                    for h in range(4):
                        k.mm(DN[:, h:h + 1], qkT[:, h, cols], nvb[:, h:h + 1], True, False, [b_qkT[h], b_nvb], [b_DN])
                        k.mm(DN[:, h:h + 1], STm[:, j, h, :], eab[:, j * 4 + h: j * 4 + h + 1], False, True,
                             [b_STm[j], b_gat], [b_DN])
                    for h in range(4):
                        k.ts("dve", CT[:, h, :], CT[:, h, :], dec[:, j * 4 + h: j * 4 + h + 1], None, ALU.mult, None,
                             [b_CT, b_gat], [b_CT])
                        k.stt(CT[:, h, :], DC[:, h, :], sd[:, j * 4 + h: j * 4 + h + 1], CT[:, h, :], ALU.mult, ALU.add,
                              [b_bk3, b_gat, b_CT], [b_CT])
                    k.tt("dve", nv[:], nv[:], dec[:, j * 4:(j + 1) * 4], ALU.mult, [b_nv, b_gat], [b_nv])
                    k.tt("dve", nvt[:], DNs, sd[:, j * 4:(j + 1) * 4], ALU.mult, [b_DNs, b_gat], [b_nv])
                    k.tt("dve", nv[:], nv[:], nvt[:], ALU.add, [b_nv], [b_nv])
                    ms_ = mst[:, a, :, :]
                    bm = b_mst[a]
                    k.act(ms_[:, 0, :], DN, AF.Abs, [b_DN], [bm])
                    k.tt("dve", ms_[:, 1, :], ms_[:, 0, :], eb[:, j * 4:(j + 1) * 4], ALU.max, [bm, b_gat], [bm])
                    k.tt("dve", ms_[:, 2, :], ms_[:, 1, :], ms_[:, 1, :], ALU.mult, [bm], [bm])
                    for h in range(4):
                        k.act(junk[:, h * 128:(h + 1) * 128], OE[:, h, :], AF.Square, [b_bk2], [b_junk, bm],
                              accum=ms_[:, 3, h:h + 1])
                    k.ts("dve", ms_[:, 4, :], ms_[:, 3, :], 1.0 / 128.0, None, ALU.mult, None, [bm], [bm])
                    k.stt(ms_[:, 5, :], ms_[:, 2, :], EPS, ms_[:, 4, :], ALU.mult, ALU.add, [bm], [bm])
                    k.tt("pool", ms_[:, 6, :], ms_[:, 5, :], neghalf[:, 0:4], ALU.pow, [bm] + RC, [bm])
                    k.cp("act", CTb[:], CT[:], [b_CT], [b_CTb])
                    k.cp("dve", nvb[:], nv[:], [b_nv], [b_nvb])
                    for h in range(4):
                        k.stt(mix[:, a, h * 128:(h + 1) * 128], OE[:, h, :], ms_[:, 6, h:h + 1],
                              gso[:, j, h * 128:(h + 1) * 128], ALU.mult, ALU.mult, [b_bk2, bm, b_gso[j]], [b_mixm[a]])
                    emit_sc(2)
                    if j > 0:
                        o_transposes(j - 1)
                    emit_sc(3)
                    if j > 0:
                        o_rest(j - 1)
                    for g in range(2):
                        for r in range(4):
                            hh = g * 4 + r
                            for ii, (kb, _) in enumerate(kbs):
                                pi = pidx[(g, kb)]
                                k.mm(SO[:, hh, :], pT[:, pi, r * 128:(r + 1) * 128], vat[:, kb, g * 64:(g + 1) * 64],
                                     ii == 0, ii == len(kbs) - 1, [b_pT[pi], b_vat[kb]], [b_bk1])
                            for ii, (kb, _) in enumerate(kbs):
                                pi = pidx[(g, kb)]
                                k.mm(SD[:, hh:hh + 1], pT[:, pi, r * 128:(r + 1) * 128], onesb[:, 0:1],
                                     ii == 0, ii == len(kbs) - 1, [b_pT[pi]] + RC, [b_SD])
                    as_ = ast[:, a, :]
                    ba = b_ast[a]
                    ob = osb[:, 0, :]
                    k.tt("dve", as_[:, 0:8], SD, esink[:], ALU.add, [b_SD] + RC, [ba])
                    k.recip(as_[:, 8:16], as_[:, 0:8], [ba], [ba])
                    k.tt("dve", ob.rearrange("p (h d) -> p h d", d=64), SO,
                         as_[:, 8:16].unsqueeze(2).broadcast_to([128, 8, 64]), ALU.mult, [b_bk1, ba], [b_osb[a]])
                    k.act(junk[:, 512:1024], ob, AF.Square, [b_osb[a]], [b_junk, ba], accum=as_[:, 16:17])
                    k.ts("dve", as_[:, 17:18], as_[:, 16:17], 1.0 / 512.0, EPS, ALU.mult, ALU.add, [ba], [ba])
                    k.tt("pool", as_[:, 18:19], as_[:, 17:18], neghalf[:, 0:1], ALU.pow, [ba] + RC, [ba])
                    k.stt(mix[:, a, 512:1024], ob, as_[:, 18:19], attg[:], ALU.mult, ALU.mult, [b_osb[a], ba] + RC, [b_mixa[a]])
                o_transposes(3)
                o_rest(3)

            k.fence("sp", x1_bufs + b_gsc[0] + b_gsc[1] + [b_G, b_G2])
            with ExitStack() as sems, nc.Block() as block:
                P.emit(block, sems)
            stats1 = P.stats

        with ExitStack() as ph:
            def sb2(name, shape, dt=F32):
                return ph.enter_context(nc.sbuf_tensor("t_" + name, list(shape), dt))

            P = Prog(nc)
            k = K(P)
            B = lambda n: Buf(n)
            Wup = sb2("Wup", [128, 8, 2 * DFF], BF16)
            Wdn = sb2("Wdn", [128, NCH, D], BF16)
            NCG = (NCH + 3) // 4
            b_Wup = [[B(f"Wup{h}_{g}") for g in range(NCG)] for h in range(2)]
            b_Wdn = [B(f"Wdn{i}") for i in range(NCH)]
            def load_wup(cg):
                c_lo, c_hi = cg * 4, min(NCH, cg * 4 + 4)
                w_ = (c_hi - c_lo) * 128
                for h in range(2):
                    col0 = h * DFF + c_lo * 128
                    k.dma("pool", Wup[:, :, col0:col0 + w_],
                          wup_d[:, col0:col0 + w_].rearrange("(k p) n -> p k n", p=128), (), [b_Wup[h][cg]])

            def load_rest_weights():
                for cg in range(1, NCG):
                    load_wup(cg)

            def load_wdn(i):
                k.dma("pool", Wdn[:, i, :], wdn_d[i * 128:(i + 1) * 128, :], (), [b_Wdn[i]])

            load_wup(0)
            RC = [B("c_ffw"), B("c_ffb")]
            ffw = sb2("ffw", [128, 2 * NCH, 3])
            ffb = sb2("ffb", [128, 2 * NCH])
            k.dma("sp", ffw[:], ffw_d, (), [RC[0]])
            k.dma("sp", ffb[:], ffb_d, (), [RC[1]])
            XR = sb2("XR", [128, 2, D])
            b_XR = [B("XR0"), B("XR1")]
            XO = sb2("XO", [128, 2, D])
            b_XO = [B("XO0"), B("XO1")]
            GP2 = sb2("GP2", [128, 1, D])
            b_GP2 = B("GP2")
            hT2 = sb2("hT2", [128, 8, 512], BF16)
            b_hT2 = [B(f"hT2{j}") for j in range(4)]
            aT = sb2("aT", [128, NCH, 512], BF16)
            b_aT = [B(f"aT{c}") for c in range(NCH)]
            xn2 = sb2("xn2", [128, D], BF16)
            b_xn2 = B("xn2")
            junk2 = sb2("junk2", [128, D], BF16)
            b_junk2 = B("junk2")
            Ug = sb2("Ug", [128, 514])
            Uv = sb2("Uv", [128, 514])
            accg = sb2("accg", [128, 512])
            accv = sb2("accv", [128, 512])
            gl = sb2("gl", [128, 512])
            b_Ug, b_Uv, b_accg, b_accv, b_gl = B("Ug"), B("Uv"), B("accg"), B("accv"), B("gl")
            halo2 = sb2("halo2", [128, 2 * NCH, 2])
            b_halo2 = B("halo2")
            st2 = sb2("st2", [128, 16])
            b_st2 = B("st2")
            yst2 = sb2("yst2", [128, 8])
            b_yst2 = B("yst2")

            TR2 = banks[0][:].bitcast(BF16).rearrange("p (k n) -> p k n", k=8)
            b_tr2 = Buf("tr2", excl=True)
            PG = [banks[1][:], banks[3][:]]
            PV = [banks[2][:], banks[4][:]]
            b_PG = [Buf("PG0", excl=True), Buf("PG1", excl=True)]
            b_PV = [Buf("PV0", excl=True), Buf("PV1", excl=True)]
            Y2 = [banks[5][:], banks[6][:]]
            b_Y2 = [Buf("Y20", excl=True), Buf("Y21", excl=True)]

            out_bufs = []
            tiles = [(b, ti) for b in range(NSEQ) for ti in range(TPS)]

            xr_loaded = set()
            xo_loaded = set()

            def load_xr(tix, j):
                b_, ti_ = tiles[tix]
                t0_ = b_ * S + ti_ * 512 + j * 128
                s_ = (tix * 4 + j) % 2
                k.dma("sp", XR[:, s_, :], x1s[t0_:t0_ + 128, :], (), [b_XR[s_]])
                xr_loaded.add((tix, j))

            def load_xo(tix, j):
                b_, ti_ = tiles[tix]
                t0_ = b_ * S + ti_ * 512 + j * 128
                s_ = (tix * 4 + j) % 2
                k.dma("sp", XO[:, s_, :], x1s[t0_:t0_ + 128, :], (), [b_XO[s_]])
                xo_loaded.add((tix, j))

            def norm_a(tix, j):
                b, ti = tiles[tix]
                t0 = b * S + ti * 512 + j * 128
                s = (tix * 4 + j) % 2
                if (tix, j) not in xr_loaded:
                    load_xr(tix, j)
                xr = XR[:, s, :]
                k.act(junk2[:], xr, AF.Square, [b_XR[s]], [b_junk2, b_st2], accum=st2[:, 0:1])
                k.ts("dve", st2[:, 1:2], st2[:, 0:1], 1.0 / D, EPS, ALU.mult, ALU.add, [b_st2], [b_st2])
                k.tt("pool", st2[:, 2:3], st2[:, 1:2], neghalf[:, 0:1], ALU.pow, [b_st2], [b_st2])
                k.act(xn2[:], xr, AF.Identity, [b_XR[s], b_st2], [b_xn2], scale=st2[:, 2:3])

            def norm_b(tix, j):
                b, ti = tiles[tix]
                for kk in range(8):
                    k.tr(TR2[:, kk, :], xn2[:, kk * 128:(kk + 1) * 128], identb[:], [b_xn2], [b_tr2])
                for kk in range(8):
                    k.ts("dve", hT2[:, kk, j * 128:(j + 1) * 128], TR2[:, kk, :], G2[:, kk, b:b + 1], SH2[:, kk, b:b + 1],
                         ALU.mult, ALU.add, [b_tr2], [b_hT2[j]])

            def norm_sub(tix, j):
                norm_a(tix, j)
                norm_b(tix, j)

            b_bank7 = Buf("bk7_2", excl=True)
            Y2sets = [((banks[7][:], banks[1][:]), (b_bank7, b_PG[0])), ((Y2[0], Y2[1]), (b_Y2[0], b_Y2[1]))]
            yst2d = sb2("yst2d", [128, 2, 8])
            b_yst2d = [B("yst2d0"), B("yst2d1")]

            for j in range(4):
                norm_sub(0, j)
            load_rest_weights()
            for tix, (b, ti) in enumerate(tiles):
                if ti == 0:
                    k.dma("sp", GP2[:], gsc[1, b:b + 1, :].partition_broadcast(128), (), [b_GP2])
                    k.memset("pool", halo2[:], 0.0, [b_halo2])
                for c in range(NCH):
                    s = c % 2
                    for kk in range(8):
                        k.mm(PG[s], Wup[:, kk, c * 128:(c + 1) * 128], hT2[:, kk, :], kk == 0, kk == 7,
                             b_hT2 + [b_Wup[0][c // 4]], [b_PG[s]])
                    for kk in range(8):
                        k.mm(PV[s], Wup[:, kk, DFF + c * 128: DFF + (c + 1) * 128], hT2[:, kk, :], kk == 0, kk == 7,
                             b_hT2 + [b_Wup[1][c // 4]], [b_PV[s]])
                    for (Ux, bU, ps, bps, ac, bac, ci) in ((Ug, b_Ug, PG[s], b_PG[s], accg, b_accg, c),
                                                           (Uv, b_Uv, PV[s], b_PV[s], accv, b_accv, NCH + c)):
                        k.cp("act", Ux[:, 2:514], ps, [bps], [bU])
                        k.cp("pool", Ux[:, 0:2], halo2[:, ci, :], [b_halo2], [bU])
                        k.cp("pool", halo2[:, ci, :], Ux[:, 512:514], [bU], [b_halo2])
                        k.act(ac[:], ps, AF.Identity, [bps] + RC, [bac], scale=ffw[:, ci, 2:3], bias=ffb[:, ci:ci + 1])
                        for tap in (1, 0):
                            k.stt(ac[:], Ux[:, tap:tap + 512], ffw[:, ci, tap:tap + 1], ac[:], ALU.mult, ALU.add,
                                  [bU, bac] + RC, [bac])
                    if tix == 0 and c < NCH // 2:
                        load_wdn(2 * c)
                        load_wdn(2 * c + 1)
                    k.act(gl[:], accg[:], AF.Gelu_apprx_tanh, [b_accg], [b_gl])
                    k.tt("dve", aT[:, c, :], gl[:], accv[:], ALU.mult, [b_gl, b_accv], [b_aT[c]])
                for j in range(4):
                    t0 = b * S + ti * 512 + j * 128
                    s = (tix * 4 + j) % 2
                    (Ya, Yb), (bYa, bYb) = Y2sets[j % 2]
                    Yh, bYh = (Ya, Yb), (bYa, bYb)
                    ys, bys = yst2d[:, j % 2, :], b_yst2d[j % 2]
                    if (tix, j) not in xo_loaded:
                        load_xo(tix, j)
                    if j + 1 < 4:
                        load_xo(tix, j + 1)
                        if tix + 1 < len(tiles):
                            load_xr(tix + 1, j + 1)
                    if tix + 1 < len(tiles):
                        norm_a(tix + 1, j)
                    for h in range(2):
                        for c in range(NCH):
                            k.mm(Yh[h], aT[:, c, j * 128:(j + 1) * 128], Wdn[:, c, h * 512:(h + 1) * 512], c == 0, c == NCH - 1,
                                 [b_aT[c], b_Wdn[c]], [bYh[h]])
                    if tix + 1 < len(tiles):
                        norm_b(tix + 1, j)
                    for h in range(2):
                        k.act(junk2[:, h * 512:(h + 1) * 512], Yh[h], AF.Square, [bYh[h]], [b_junk2, bys],
                              accum=ys[:, h:h + 1])
                    k.tt("dve", ys[:, 2:3], ys[:, 0:1], ys[:, 1:2], ALU.add, [bys], [bys])
                    k.ts("dve", ys[:, 3:4], ys[:, 2:3], 1.0 / D, EPS, ALU.mult, ALU.add, [bys], [bys])
                    k.tt("pool", ys[:, 4:5], ys[:, 3:4], neghalf[:, 0:1], ALU.pow, [bys], [bys])
                    xo = XO[:, s, :]
                    for h in range(2):
                        k.stt(Yh[h], Yh[h], ys[:, 4:5], GP2[:, 0, h * 512:(h + 1) * 512], ALU.mult, ALU.mult,
                              [bYh[h], bys, b_GP2], [bYh[h]])
                        k.tt("dve", xo[:, h * 512:(h + 1) * 512], Yh[h], xo[:, h * 512:(h + 1) * 512], ALU.add,
                             [bYh[h], b_XO[s]], [b_XO[s]])
                    bo = B(f"o{t0}")
                    out_bufs.append(bo)
                    k.dma("sp", out_d[t0:t0 + 128, :], xo, [b_XO[s]], [bo])
            k.fence("sp", out_bufs)
            with ExitStack() as sems, nc.Block() as block:
                P.emit(block, sems)
            stats2 = P.stats
    nc._stats = (stats1, stats2)
    return nc


def _consts(NSEQ, S):
    NB = S // 128
    bf = ml_dtypes.bfloat16
    p = np.arange(128)
    identb = np.eye(128, dtype=np.float32).astype(bf)
    triu = (p[:, None] <= p[None, :]).astype(np.float32)
    masku = triu * np.float32(KSCALE)
    mc = np.where(p[:, None] <= p[None, :], 0.0, NEG).astype(np.float32)
    mp = np.where(p[:, None] > p[None, :], 0.0, NEG).astype(np.float32)
    maskc = np.tile(mc, (1, 4)).astype(bf)
    maskp = np.tile(mp, (1, 4)).astype(bf)
    half = 32
    inv = (10000.0 ** (-np.arange(half, dtype=np.float32) / half)).astype(np.float32)
    pos = (np.arange(NB)[None, :] * 128 + p[:, None]).astype(np.float32)
    ang = pos[:, :, None] * inv[None, None, :]
    return dict(identb=identb, triu=triu, masku=masku, maskc=maskc, maskp=maskp,
                cos=np.cos(ang).astype(np.float32), sin=np.sin(ang).astype(np.float32),
                i4=np.eye(4, dtype=np.float32))


def _fp(v, nk):
    return np.ascontiguousarray(np.asarray(v, np.float32).reshape(nk, 128).T)


def _rb(v):
    v = np.asarray(v, np.float32).reshape(1, -1)
    return np.ascontiguousarray(np.broadcast_to(v, (128, v.shape[1])))


_NC_CACHE = {}


def run(inputs, NSEQ, S):
    f = lambda a: np.ascontiguousarray(np.asarray(a, np.float32))
    x = f(inputs["x"])
    c = f(inputs["c"])
    assert x.shape[0] == NCORES * NSEQ and x.shape[1] == S
    key = (NSEQ, S)
    if key not in _NC_CACHE:
        _NC_CACHE[key] = build(NSEQ, S)
    nc = _NC_CACHE[key]
    cst = _consts(NSEQ, S)
    shared = dict(cst)
    shared["w_ada"] = f(inputs["w_ada"][0])
    shared["b_ada"] = f(inputs["b_ada"][0]).reshape(1, -1)
    shared["w_in"] = f(inputs["w_in"][0])
    shared["w_out"] = f(inputs["w_out"][0])
    shared["w_up"] = f(inputs["w_up"][0])
    shared["w_down"] = f(inputs["w_down"][0])
    shared["pmg"] = _fp(inputs["pre_mix_g"][0], 8)
    shared["pfg"] = _fp(inputs["pre_ffn_g"][0], 8)
    post = np.stack([f(inputs["post_mix_g"][0]), f(inputs["post_ffn_g"][0])], 0)
    shared["post4"] = np.ascontiguousarray(np.broadcast_to(post[None], (NSEQ, 2, D)))
    shared["mlcw"] = np.ascontiguousarray(f(inputs["ml_conv_w"][0]).reshape(4, 8, 128).transpose(2, 1, 0))
    shared["mlcb"] = _fp(inputs["ml_conv_b"][0], 8)
    shared["ib16"] = _rb(np.tile(f(inputs["ml_i_b"][0]), 4))
    shared["fb16"] = _rb(np.tile(f(inputs["ml_f_b"][0]), 4))
    shared["mlg"] = _rb(f(inputs["ml_norm_g"][0]).reshape(-1))
    shared["sinks"] = _rb(inputs["attn_sinks"][0])
    shared["attg"] = _rb(inputs["attn_norm_g"][0])
    shared["ffw"] = np.ascontiguousarray(f(inputs["ffn_conv_w"][0]).reshape(3, 2 * NCH, 128).transpose(2, 1, 0))
    shared["ffb"] = _fp(inputs["ffn_conv_b"][0], 2 * NCH)
    in_maps = []
    for i in range(NCORES):
        m = dict(shared)
        m["x"] = np.ascontiguousarray(x[i * NSEQ:(i + 1) * NSEQ].reshape(NSEQ * S, D))
        ci = c[i * NSEQ:(i + 1) * NSEQ]
        m["cT"] = np.ascontiguousarray(ci.reshape(NSEQ, 8, 128).transpose(2, 1, 0))
        in_maps.append(m)
    res = run_bass_kernel_spmd(nc, in_maps, core_ids=list(range(NCORES)))
    outs = [np.asarray(r["out"], np.float32).reshape(NSEQ, S, D) for r in res.results]
    return np.concatenate(outs, axis=0)


def kernel(**inputs):
    x = inputs["x"]
    return run(inputs, x.shape[0] // NCORES, x.shape[1])
```
